# Optimizing a Trainium2 kernel written in Bass

```python
import jax, jax.numpy as jnp
from jax import lax
import numpy as np

D_MODEL = 1024
BATCH = 8
SEQ = 2048
DEPTH = 4

EPS_RMS = 1e-6
EPS_LN = 1e-5
W_CONV = D_MODEL
CONV_K = 31
W_POOL = D_MODEL
POOL_WINDOWS = (2, 4, 8, 16)
POOL_GROUPS = len(POOL_WINDOWS)
POOL_GW = W_POOL // POOL_GROUPS
W_EVEN_IN = 3 * W_CONV + 2 * W_POOL
W_EVEN_MIX = W_CONV + W_POOL
LRU_HEADS = 12
LRU_HD = 128
W_LRU = LRU_HEADS * LRU_HD
LRU_CONV_K = 4
LRU_C = 8.0
N_EVEN = (DEPTH + 1) // 2
N_ODD = DEPTH // 2

kernel_name = "hybrid_conv_pool_rglru_trunk"


def rmsnorm(x, g):
    xf = x.astype(jnp.float32)
    y = xf * lax.rsqrt(jnp.mean(xf * xf, axis=-1, keepdims=True) + EPS_RMS)
    return (y * g.astype(jnp.float32)).astype(x.dtype)


def layernorm(x, g, b):
    xf = x.astype(jnp.float32)
    mu = jnp.mean(xf, axis=-1, keepdims=True)
    var = jnp.mean(jnp.square(xf - mu), axis=-1, keepdims=True)
    y = (xf - mu) * lax.rsqrt(var + EPS_LN)
    return (y * g.astype(jnp.float32) + b.astype(jnp.float32)).astype(x.dtype)


def causal_depthwise_conv(x, w, b):
    k = w.shape[0]
    y = lax.conv_general_dilated(
        x, w[:, None, :].astype(x.dtype), window_strides=(1,), padding=[(k - 1, 0)],
        dimension_numbers=("NWC", "WIO", "NWC"), feature_group_count=x.shape[-1])
    return y + b.astype(x.dtype)


def multiscale_pool_diff(v):
    bsz, t, _ = v.shape
    vf = v.astype(jnp.float32)
    cs_pad = jnp.pad(jnp.cumsum(vf, axis=1), ((0, 0), (1, 0), (0, 0)))
    pos = jnp.arange(1, t + 1, dtype=jnp.float32)
    outs = []
    for g, w in enumerate(POOL_WINDOWS):
        seg = cs_pad[:, :, g * POOL_GW:(g + 1) * POOL_GW]
        upper = seg[:, 1:]
        lower = jnp.pad(seg[:, :t - w + 1], ((0, 0), (w - 1, 0), (0, 0)))
        cnt = jnp.minimum(pos, jnp.float32(w))[None, :, None]
        outs.append((upper - lower) / cnt)
    return jnp.concatenate(outs, axis=-1) - vf


def even_mixer(h, w_in, conv_w, conv_b, ln_g, ln_b, pool_w, pool_b, pool_scale, w_out):
    bsz, t, _ = h.shape
    p = jnp.einsum("btd,dn->btn", h, w_in)
    a_val, a_glu, a_gate, b_val, b_gate = jnp.split(
        p, [W_CONV, 2 * W_CONV, 3 * W_CONV, 3 * W_CONV + W_POOL], axis=-1)
    u = a_val * jax.nn.sigmoid(a_glu)
    u = causal_depthwise_conv(u, conv_w, conv_b)
    u = jax.nn.silu(layernorm(u, ln_g, ln_b))
    ya = u * jax.nn.silu(a_gate)
    d = multiscale_pool_diff(b_val).reshape(bsz, t, POOL_GROUPS, POOL_GW)
    d = jnp.einsum("btgc,gce->btge", d, pool_w.astype(jnp.float32)) + pool_b.astype(jnp.float32)
    yb = (d.reshape(bsz, t, W_POOL) * pool_scale.astype(jnp.float32)).astype(h.dtype)
    yb = yb * jax.nn.silu(b_gate)
    y = jnp.concatenate([ya, yb], axis=-1)
    return jnp.einsum("btn,nd->btd", y, w_out)


def _lin_combine(c1, c2):
    a1, b1 = c1
    a2, b2 = c2
    return a1 * a2, a2 * b1 + b2


def odd_mixer(h, w_in, conv_w, conv_b, w_rg, b_rg, w_ig, b_ig, lam, w_out):
    bsz, t, _ = h.shape
    p = jnp.einsum("btd,dn->btn", h, w_in)
    xr, gate = jnp.split(p, [W_LRU], axis=-1)
    xc = causal_depthwise_conv(xr, conv_w, conv_b)
    xh = xc.reshape(bsz, t, LRU_HEADS, LRU_HD)
    r = jax.nn.sigmoid(jnp.einsum("bthi,hij->bthj", xh, w_rg).reshape(bsz, t, W_LRU).astype(jnp.float32)
                       + b_rg.astype(jnp.float32))
    i = jax.nn.sigmoid(jnp.einsum("bthi,hij->bthj", xh, w_ig).reshape(bsz, t, W_LRU).astype(jnp.float32)
                       + b_ig.astype(jnp.float32))
    log_a = -LRU_C * r * jax.nn.softplus(-lam.astype(jnp.float32))
    a = jnp.exp(log_a)
    mult = jnp.sqrt(-jnp.expm1(2.0 * log_a))
    bterm = mult * (i * xc.astype(jnp.float32))
    _, hs = lax.associative_scan(_lin_combine, (a, bterm), axis=1)
    y = hs.astype(h.dtype) * jax.nn.silu(gate)
    return jnp.einsum("btn,nd->btd", y, w_out)


def setup_inputs(seed: int = 0) -> dict:
    key = jax.random.key(seed)
    ks = iter(jax.random.split(key, 32))
    f32 = jnp.float32

    def nrm(shape, scale):
        return jax.random.normal(next(ks), shape, f32) * scale

    x = jax.random.normal(next(ks), (BATCH, SEQ, D_MODEL), f32)
    a0 = jax.random.uniform(next(ks), (N_ODD, W_LRU), f32, 0.9, 0.999)
    s = a0 ** (1.0 / LRU_C)
    lru_lambda = jnp.log(s) - jnp.log1p(-s)
    return {
        "x": x,
        "norm_even": 1.0 + nrm((N_EVEN, D_MODEL), 0.02),
        "w_in_even": nrm((N_EVEN, D_MODEL, W_EVEN_IN), D_MODEL ** -0.5),
        "conv_a_w": nrm((N_EVEN, CONV_K, W_CONV), CONV_K ** -0.5),
        "conv_a_b": nrm((N_EVEN, W_CONV), 0.01),
        "ln_a_g": 1.0 + nrm((N_EVEN, W_CONV), 0.02),
        "ln_a_b": nrm((N_EVEN, W_CONV), 0.01),
        "pool_w": nrm((N_EVEN, POOL_GROUPS, POOL_GW, POOL_GW), POOL_GW ** -0.5),
        "pool_b": nrm((N_EVEN, POOL_GROUPS, POOL_GW), 0.01),
        "pool_scale": 1.0 + nrm((N_EVEN, W_POOL), 0.1),
        "w_out_even": nrm((N_EVEN, W_EVEN_MIX, D_MODEL), W_EVEN_MIX ** -0.5),
        "norm_odd": 1.0 + nrm((N_ODD, D_MODEL), 0.02),
        "w_in_odd": nrm((N_ODD, D_MODEL, 2 * W_LRU), D_MODEL ** -0.5),
        "conv_c_w": nrm((N_ODD, LRU_CONV_K, W_LRU), LRU_CONV_K ** -0.5),
        "conv_c_b": nrm((N_ODD, W_LRU), 0.01),
        "w_rg": nrm((N_ODD, LRU_HEADS, LRU_HD, LRU_HD), LRU_HD ** -0.5),
        "b_rg": nrm((N_ODD, W_LRU), 0.01),
        "w_ig": nrm((N_ODD, LRU_HEADS, LRU_HD, LRU_HD), LRU_HD ** -0.5),
        "b_ig": nrm((N_ODD, W_LRU), 0.01),
        "lru_lambda": lru_lambda,
        "w_out_odd": nrm((N_ODD, W_LRU, D_MODEL), W_LRU ** -0.5),
        "final_norm": 1.0 + nrm((D_MODEL,), 0.02),
    }


def reference(x, norm_even, w_in_even, conv_a_w, conv_a_b, ln_a_g, ln_a_b, pool_w, pool_b,
              pool_scale, w_out_even, norm_odd, w_in_odd, conv_c_w, conv_c_b, w_rg, b_rg,
              w_ig, b_ig, lru_lambda, w_out_odd, final_norm):
    h = x
    for layer in range(DEPTH):
        if layer % 2 == 0:
            j = layer // 2
            h = h + even_mixer(rmsnorm(h, norm_even[j]), w_in_even[j], conv_a_w[j], conv_a_b[j],
                               ln_a_g[j], ln_a_b[j], pool_w[j], pool_b[j], pool_scale[j],
                               w_out_even[j])
        else:
            j = layer // 2
            h = h + odd_mixer(rmsnorm(h, norm_odd[j]), w_in_odd[j], conv_c_w[j], conv_c_b[j],
                              w_rg[j], b_rg[j], w_ig[j], b_ig[j], lru_lambda[j], w_out_odd[j])
    return rmsnorm(h, final_norm)
```

```python
import numpy as np
from contextlib import ExitStack
import concourse.bass as bass
import concourse.mybir as mybir
from concourse.bass_utils import run_bass_kernel_spmd

F32 = mybir.dt.float32
BF16 = mybir.dt.bfloat16
F32R = mybir.dt.float32r
I32 = mybir.dt.int32
AF = mybir.ActivationFunctionType
ALU = mybir.AluOpType

D = 1024
T = 2048
TH = 1024
NT = 512
KC = 8
EPS_RMS = 1e-6
EPS_LN = 1e-5
CONV_K = 31
POOL_WINDOWS = (2, 4, 8, 16)
NHEAD = 12
DVE_TAPS = 8

E_NORM, E_CONVB, E_LNG, E_LNB, E_POOLB, E_POOLS, E_CONVW = 0, 8, 16, 24, 32, 40, 48
E_SIZE = 48 + 8 * 32
O_NORM, O_CCB, O_BRG, O_BIG, O_LAM, O_CCW = 0, 8, 20, 32, 44, 56
O_SIZE = 56 + 48
EB = [0, E_SIZE]
OB = [2 * E_SIZE, 2 * E_SIZE + O_SIZE]
FN_BASE = 2 * E_SIZE + 2 * O_SIZE
NCV = FN_BASE + 8


class Sem:
    def __init__(self, handle, inc):
        self.h = handle
        self.inc = inc
        self.count = 0


class Prog:
    ENGS = ("pe", "act", "dve", "pool", "sp")

    def __init__(self, nc, stack):
        self.nc = nc
        self.stack = stack
        self.lists = {e: [] for e in self.ENGS}
        self.esem = {e: Sem(stack.enter_context(nc.semaphore("s_" + e)), 1) for e in self.ENGS}
        self.waited = {e: {} for e in self.ENGS}
        self.last_write = {}
        self.readers = {}
        self.region_of = {}

    def new_dma_sem(self, name):
        return Sem(self.stack.enter_context(self.nc.semaphore("d_" + name)), 16)

    def _deps(self, reads, writes, own=None):
        waits = {}

        def need(s, v):
            if waits.get(s, 0) < v:
                waits[s] = v

        reads = list(reads)
        for k in list(reads) + list(writes):
            rk = self.region_of.get(k[0]) if isinstance(k, tuple) else None
            if rk is not None and rk not in reads:
                reads.append(rk)
        for k in reads:
            for (s, v) in self.last_write.get(k, ()):
                need(s, v)
            if isinstance(k, tuple) and k[0] == "ps":
                for (s, v) in self.readers.get(k, {}).items():
                    if s is not own:
                        need(s, v)
        for k in writes:
            for (s, v) in self.last_write.get(k, ()):
                need(s, v)
            for (s, v) in self.readers.get(k, {}).items():
                need(s, v)
        return reads, waits

    def op(self, eng, fn, reads=(), writes=(), dsem=None, n=1):
        done = dsem if dsem is not None else self.esem[eng]
        reads, waits = self._deps(reads, writes, own=done)
        wl = []
        for s, v in waits.items():
            if eng == "pe" and s is self.esem["pe"]:
                continue
            if self.waited[eng].get(s, 0) >= v:
                continue
            self.waited[eng][s] = v
            wl.append((s.h, v))
        done.count += done.inc * n
        val = done.count
        sh, inc = done.h, done.inc

        def emit(e, wl=wl, fn=fn, sh=sh, inc=inc, n=n):
            for (h, v) in wl:
                e.wait_ge(h, v)
            r = fn(e)
            if isinstance(r, (list, tuple)):
                assert len(r) == n
                for x in r:
                    x.then_inc(sh, inc)
            else:
                assert n == 1
                r.then_inc(sh, inc)

        self.lists[eng].append(emit)
        for k in writes:
            self.last_write[k] = [(done, val)]
            self.readers[k] = {}
        for k in reads:
            self.readers.setdefault(k, {})[done] = val
        return val

    def fence(self, key):
        cur = {}
        for (s, v) in self.last_write.get(key, ()):
            cur[s] = max(cur.get(s, 0), v)
        for (s, v) in self.readers.get(key, {}).items():
            cur[s] = max(cur.get(s, 0), v)
        self.last_write[key] = list(cur.items())
        self.readers[key] = {}

    def final_wait(self, eng, sems):
        wl = [(s.h, s.count) for s in sems if s.count > 0]

        def emit(e, wl=wl):
            for (h, v) in wl:
                e.wait_ge(h, v)

        self.lists[eng].append(emit)

    def replay(self, block):
        L = self.lists

        @block.tensor
        def _(e):
            for f in L["pe"]:
                f(e)

        @block.scalar
        def _(e):
            for f in L["act"]:
                f(e)

        @block.vector
        def _(e):
            for f in L["dve"]:
                f(e)

        @block.gpsimd
        def _(e):
            for f in L["pool"]:
                f(e)

        @block.sync
        def _(e):
            for f in L["sp"]:
                f(e)


def build_program(layers, do_final_norm):
    nc = bass.Bass("TRN2", target_bir_lowering=False)
    dr = lambda name, shape, kind="ExternalInput": nc.dram_tensor(name, shape, F32, kind=kind).ap()
    xin = dr("xin", [D, T])
    cvec_d = dr("cvec", [128, NCV])
    ctab_d = dr("ctab", [128, 64])
    wie = dr("wie", [2, 40, 128, 1024])
    woe = dr("woe", [2, 8, 128, 2048])
    wio = dr("wio", [2, 24, 128, 1024])
    woo = dr("woo", [2, 8, 128, 1536])
    pwd = dr("pw", [2, 4, 128, 512])
    wgt = dr("wgt", [2, 12, 128, 256])
    outd = dr("out", [D, T], kind="ExternalOutput")

    with ExitStack() as st:
        P = Prog(nc, st)
        sb = lambda name, shape, dt=F32: st.enter_context(nc.sbuf_tensor(name, shape, dt))
        H = sb("H", [128, KC, T])
        HN = sb("HN", [128, KC, TH], BF16)
        R = sb("R", [128, 13384])
        ST = sb("ST", [128, 3, TH])
        WS = [sb("WS%d" % i, [128, 2560], BF16) for i in range(3)]
        CV = sb("CV", [128, 2 * CONV_K * 128], BF16)
        DG = [CV[:, i * CONV_K * 128:(i + 1) * CONV_K * 128].rearrange("p (k j) -> p k j", k=CONV_K) for i in range(2)]
        SST = CV[:, 0:4 * 1056].rearrange("p (j x) -> p j x", j=4)
        LT = [CV[:, 4224 + i * 1024:4224 + (i + 1) * 1024].rearrange("p (q c) -> p q c", q=32) for i in range(2)]
        UB = [sb("UB%d" % i, [128, 40 + TH], BF16) for i in range(2)]
        I4 = sb("I4", [128, 32], BF16)
        TMP = sb("TMP", [128, 4, TH])
        cvec = sb("cvec_s", [128, NCV])
        ctab = sb("ctab_s", [128, 64])
        iot = sb("iot", [128, 128], I32)
        ident = sb("ident", [128, 128], BF16)
        onesD = sb("onesD", [128, 128])
        onesR = sb("onesR", [128, 128], F32R)
        SQR = sb("SQR", [128, TH], F32R)
        UHALO = sb("UHALO", [128, 8, 32], BF16)
        VHALO = sb("VHALO", [128, 8, 16])
        XHALO = sb("XHALO", [128, NHEAD, 4], BF16)
        CARRY = sb("CARRY", [128, NHEAD])
        DER = sb("DER", [128, 192])
        PSALL = st.enter_context(nc.psum_tensor("psall", [128, 8, NT], F32))

        Uv = lambda ca: R[:, ca * TH:(ca + 1) * TH]
        YAv = lambda ca: R[:, ca * TH:ca * TH + TH // 2].bitcast(BF16)
        YBv = lambda cb: R[:, 8256 + cb * 512:8256 + (cb + 1) * 512].bitcast(BF16)
        VPv = lambda hf: R[:, hf * 1040:(hf + 1) * 1040]
        SAv = R[:, 2080:3120]
        SBv = R[:, 3120:4160]
        SGBv = lambda gp, hf: R[:, 4160 + (2 * gp + hf) * TH:4160 + (2 * gp + hf + 1) * TH]
        Dv = lambda hf: R[:, 12352 + hf * 512:12352 + (hf + 1) * 512].bitcast(BF16)
        Yv = lambda m: R[:, m * 512:(m + 1) * 512].bitcast(BF16)
        Av = lambda b: R[:, 6144 + b * TH:6144 + (b + 1) * TH]
        HSv = R[:, 8192:9216]
        SGv = lambda b4: R[:, 9216 + b4 * TH:9216 + (b4 + 1) * TH]
        XRv = lambda b: DG[b][:, 20:29, :].rearrange("p a b -> p (a b)")[:, 0:1032]
        TXv = lambda b: DG[b][:, 4:20, :].rearrange("p a b -> p (a b)").bitcast(F32)
        XCBv = lambda b: UB[b][:, 0:TH]
        for nm in ("U", "YA", "YB", "VP", "SA", "SB", "SGB", "Dp", "Y", "A", "HS", "SG"):
            P.region_of[nm] = ("REG", "R")
        for nm in ("DG", "TX", "UB", "XCB", "XR", "SST", "LT"):
            P.region_of[nm] = ("REG", "E")

        tl = lambda n: slice(n * NT, (n + 1) * NT)
        cv = lambda col: cvec[:, col:col + 1]
        dc = lambda col: DER[:, col:col + 1]
        psb = lambda p, n: PSALL[:, 2 * p + n, :]
        psw = lambda p: PSALL[:, 2 * p:2 * p + 2, :].rearrange("p a b -> p (a b)")
        pk = lambda p: [("ps", 2 * p), ("ps", 2 * p + 1)]

        state = {"pp": 0, "tmp": 0}

        def pspair():
            p = state["pp"]
            state["pp"] = (p + 1) % 4
            return p

        def tmpw():
            i = state["tmp"]
            state["tmp"] = (i + 1) % 4
            return TMP[:, i, :], ("TMP", i)

        def mm(p, n, pairs, reads):
            def fn(e):
                k = len(pairs)
                r = None
                for i, (l, rh) in enumerate(pairs):
                    r = e.matmul(psb(p, n), l, rh, start=(i == 0), stop=(i == k - 1))
                return r
            P.op("pe", fn, reads=reads, writes=[("ps", 2 * p + n)])

        def mm_part(p, n, pairs, reads, first, last):
            def fn(e):
                k = len(pairs)
                r = None
                for i, (l, rh) in enumerate(pairs):
                    r = e.matmul(psb(p, n), l, rh, start=(first and i == 0), stop=(last and i == k - 1))
                return r
            P.op("pe", fn, reads=reads, writes=[("ps", 2 * p + n)])

        def act(out, in_, func, reads, writes, bias=None, scale=None):
            kw = {}
            if bias is not None:
                kw["bias"] = bias
            if scale is not None:
                kw["scale"] = scale
            P.op("act", lambda e: e.activation(out=out, in_=in_, func=func, **kw), reads=reads, writes=writes)

        def tt(eng, out, in0, in1, op, reads, writes):
            P.op(eng, lambda e: e.tensor_tensor(out=out, in0=in0, in1=in1, op=op), reads=reads, writes=writes)

        def ts(eng, out, in0, s1, s2, op0, op1, reads, writes):
            if op1 is None:
                P.op(eng, lambda e: e.tensor_scalar(out=out, in0=in0, scalar1=s1, scalar2=None, op0=op0),
                     reads=reads, writes=writes)
            else:
                P.op(eng, lambda e: e.tensor_scalar(out=out, in0=in0, scalar1=s1, scalar2=s2, op0=op0, op1=op1),
                     reads=reads, writes=writes)

        def stt(out, in0, scalar, in1, op0, op1, reads, writes):
            P.op("dve", lambda e: e.scalar_tensor_tensor(out=out, in0=in0, scalar=scalar, in1=in1, op0=op0, op1=op1),
                 reads=reads, writes=writes)

        def cp(eng, out, in_, reads, writes):
            P.op(eng, lambda e: e.tensor_copy(out=out, in_=in_), reads=reads, writes=writes)

        def mset(eng, ap, val, writes):
            P.op(eng, lambda e: e.memset(ap, val), writes=writes)

        s_x = P.new_dma_sem("x")
        s_c = P.new_dma_sem("c")
        s_w = [P.new_dma_sem("w%d" % i) for i in range(3)]
        s_o = [P.new_dma_sem("o%d" % i) for i in range(2)]
        s_s = P.new_dma_sem("sst")
        s_s2 = P.new_dma_sem("sst2")
        P.op("sp", lambda e: [e.dma_start(out=cvec[:], in_=cvec_d[:, :]), e.dma_start(out=ctab[:], in_=ctab_d[:, :])],
             writes=["cvec", "ctab"], dsem=s_c, n=2)
        s_x2 = P.new_dma_sem("x2")
        P.op("sp", lambda e: [e.dma_start(out=H[:, kc, 0:TH], in_=xin[kc * 128:(kc + 1) * 128, 0:TH]) for kc in range(KC)],
             writes=[("H", kc, 0) for kc in range(KC)], dsem=s_x, n=KC)
        P.op("sp", lambda e: [e.dma_start(out=H[:, kc, TH:T], in_=xin[kc * 128:(kc + 1) * 128, TH:T]) for kc in range(KC)],
             writes=[("H", kc, 1) for kc in range(KC)], dsem=s_x2, n=KC)

        P.op("pool", lambda e: e.iota(iot[:], pattern=[[1, 128]], base=0, channel_multiplier=-1), writes=["iot"])
        ts("dve", ident[:], iot[:], 0.0, None, ALU.is_equal, None, ["iot"], ["ident"])
        mset("pool", onesD[:], 1.0 / D, ["onesD"])
        for q4 in range(4):
            cp("dve", I4[32 * q4:32 * q4 + 32, :], ident[32 * q4:32 * q4 + 32, 32 * q4:32 * q4 + 32], ["ident"], [("I4", q4)])
        P.last_write["I4"] = sum([P.last_write[("I4", q4)] for q4 in range(4)], [])
        for b_ in range(2):
            mset("pool", UB[b_][:, 32 + TH:40 + TH], 0.0, [("UB", b_, "tail")])
        cp("dve", onesR[:], onesD[:], ["onesD"], ["onesR"])
        mset("pool", CARRY[:], 0.0, ["CARRY"])
        mset("pool", DER[:, 112:113], EPS_RMS, ["DERe0"])
        mset("pool", DER[:, 113:114], EPS_LN, ["DERe1"])
        mset("pool", DER[:, 114:115], 0.25, ["DERe2"])
        mset("pool", DER[:, 115:116], 0.0625, ["DERe"])
        P.last_write["DERe"] = P.last_write["DERe"] + P.last_write["DERe0"] + P.last_write["DERe1"] + P.last_write["DERe2"]

        for j in range(2):
            ts("dve", DER[:, 48 + 8 * j:56 + 8 * j], cvec[:, EB[j] + E_POOLB:EB[j] + E_POOLB + 8], 1.0, None, ALU.mult, None,
               ["cvec"], ["DERa%d" % j])
            tt("dve", DER[:, 48 + 8 * j:56 + 8 * j], DER[:, 48 + 8 * j:56 + 8 * j],
               cvec[:, EB[j] + E_POOLS:EB[j] + E_POOLS + 8], ALU.mult, ["cvec", "DERa%d" % j], ["DERpbs%d" % j])
            act(DER[:, 64 + 12 * j:76 + 12 * j], cvec[:, OB[j] + O_LAM:OB[j] + O_LAM + 12], AF.Exp, ["cvec"], ["DERz%d" % j],
                scale=-1.0)
            ts("dve", DER[:, 128 + 12 * j:140 + 12 * j], cvec[:, OB[j] + O_BRG:OB[j] + O_BRG + 12], 0.5, None, ALU.mult, None,
               ["cvec"], ["DERhb"])
            ts("dve", DER[:, 152 + 12 * j:164 + 12 * j], cvec[:, OB[j] + O_BIG:OB[j] + O_BIG + 12], 0.5, None, ALU.mult, None,
               ["cvec"], ["DERhb"])
        zz = DER[:, 64:88]
        tt_ = DER[:, 88:112]
        NTERM = 10
        ts("dve", tt_, zz, -1.0 / NTERM, 1.0 / (NTERM - 1), ALU.mult, ALU.add, ["DERz0", "DERz1"], ["DERt"])
        for q in range(NTERM - 2, 0, -1):
            tt("dve", tt_, tt_, zz, ALU.mult, ["DERt", "DERz0", "DERz1"], ["DERt"])
            ts("dve", tt_, tt_, -1.0, 1.0 / q, ALU.mult, ALU.add, ["DERt"], ["DERt"])
        tt("dve", tt_, tt_, zz, ALU.mult, ["DERt", "DERz0", "DERz1"], ["DERt"])
        ts("dve", DER[:, 0:24], tt_, -8.0, None, ALU.mult, None, ["DERt"], ["DERsp"])
        ts("dve", DER[:, 24:48], tt_, -4.0, None, ALU.mult, None, ["DERt"], ["DERsp"])

        def load_w(slot, parts):
            def fn(e):
                r = []
                for pt in parts:
                    c0, nc_, src = pt[0], pt[1], pt[2]
                    if len(pt) == 4:
                        dst = WS[slot][:, c0:c0 + 2048].rearrange("p (c n) -> p c n", c=2)
                        r.append(e.dma_start(out=dst, in_=src.rearrange("c p n -> p c n")))
                    else:
                        r.append(e.dma_start(out=WS[slot][:, c0:c0 + nc_], in_=src))
                return r
            P.op("pool", fn, writes=[("W", slot)], dsem=s_w[slot], n=len(parts))

        def build_dg(buf, col0, ntap):
            def fn(e):
                in0 = ident[:].rearrange("p (o j) -> p o j", o=1).to_broadcast([128, ntap, 128])
                in1 = cvec[:, col0:col0 + ntap].rearrange("p (k o) -> p k o", o=1).to_broadcast([128, ntap, 128])
                return e.tensor_tensor(out=DG[buf][:, 0:ntap, :], in0=in0, in1=in1, op=ALU.mult)
            P.op("pool", fn, reads=["ident", "cvec"], writes=[("DG", buf)])

        def build_dg_act(buf, col0, ntap):
            def fn(e):
                return [e.activation(out=DG[buf][:, k, :], in_=ident[:], func=AF.Identity, scale=cvec[:, col0 + k:col0 + k + 1])
                        for k in range(ntap)]
            P.op("act", fn, reads=["ident", "cvec"], writes=[("DG", buf)], n=ntap)

        def rms_stats(hh):
            p = pspair()
            for kc in range(KC):
                for n in range(2):
                    act(SQR[:, tl(n)], H[:, kc, hh * TH + n * NT:hh * TH + (n + 1) * NT], AF.Square, [("H", kc, hh)], [("SQ", n)])
                for n in range(2):
                    def fn(e, kc=kc, n=n):
                        return e.matmul(psb(p, n), onesR[:], SQR[:, tl(n)], start=(kc == 0), stop=(kc == KC - 1))
                    P.op("pe", fn, reads=["onesR", ("SQ", n)], writes=[("ps", 2 * p + n)])
            t, tk = tmpw()
            act(t, psw(p), AF.Ln, pk(p) + ["DERe"], [tk], bias=dc(112))
            act(ST[:, 2, :], t, AF.Exp, [tk], ["RS"], scale=-0.5)

        def norm_half(hh, gcol):
            rms_stats(hh)
            for kc in range(KC):
                stt(HN[:, kc, :], H[:, kc, hh * TH:(hh + 1) * TH], cv(gcol + kc), ST[:, 2, :],
                    ALU.mult, ALU.mult, [("H", kc, hh), "RS", "cvec"], [("HN", kc)])

        def inproj(p, slot, c0):
            for n in range(2):
                mm(p, n, [(WS[slot][:, c0 + kc * 128:c0 + (kc + 1) * 128], HN[:, kc, tl(n)]) for kc in range(KC)],
                   [("W", slot)] + [("HN", kc) for kc in range(KC)])

        steps = []

        def even_half(j, hh):
            cb0 = EB[j]

            def pre(slot, buf):
                P.fence(("REG", "R"))
                P.fence(("REG", "E"))
            steps.append(dict(head=pre, norm=(lambda: norm_half(hh, cb0 + E_NORM)), flush=True))

            for g in (3, 2, 1, 0):
                w = POOL_WINDOWS[g]
                gp = g % 2
                for hf in range(2):
                    cb = 2 * g + hf

                    def dma(slot, g=g, hf=hf, cb=cb):
                        parts = [(0, 2048, wie[j, 16 + 2 * cb:18 + 2 * cb], "pair")]
                        if hf == 1:
                            parts.append((2048, 512, pwd[j, g]))
                        load_w(slot, parts)

                    def head(slot, buf, g=g, hf=hf, cb=cb, gp=gp):
                        VP = VPv(hf)
                        if hh == 0:
                            mset("pool", VP[:, 0:16], 0.0, [("VP", hf, "pad")])
                        else:
                            cp("pool", VP[:, 0:16], VHALO[:, cb, :], [("VHALO", cb)], [("VP", hf, "pad")])
                        pv, pg = pspair(), pspair()
                        inproj(pv, slot, 0)
                        inproj(pg, slot, 1024)
                        act(VP[:, 16:1040], psw(pv), AF.Copy, pk(pv), [("VP", hf, "d")])
                        act(SGBv(gp, hf), psw(pg), AF.Silu, pk(pg), [("SGB", gp, hf)])

                    def tail(slot, buf, g=g, hf=hf, cb=cb, w=w):
                        VP = VPv(hf)
                        vk = [("VP", hf, "pad"), ("VP", hf, "d")]
                        if hh == 0:
                            cp("pool", VHALO[:, cb, :], VP[:, 1024:1040], vk, [("VHALO", cb)])
                        tt("dve", SAv[:, 1:1040], VP[:, 1:1040], VP[:, 0:1039], ALU.add, vk, [("SA",)])
                        cur, curk = SAv, ("SA",)
                        if g >= 1:
                            tt("dve", SBv[:, 3:1040], SAv[:, 3:1040], SAv[:, 1:1038], ALU.add, [("SA",)], [("SB",)])
                            cur, curk = SBv, ("SB",)
                        if g >= 2:
                            tt("dve", SAv[:, 7:1040], SBv[:, 7:1040], SBv[:, 3:1036], ALU.add, [("SB",)], [("SA",)])
                            cur, curk = SAv, ("SA",)
                        if g >= 3:
                            tt("dve", SBv[:, 15:1040], SAv[:, 15:1040], SAv[:, 7:1032], ALU.add, [("SA",)], [("SB",)])
                            cur, curk = SBv, ("SB",)
                        Dh = Dv(hf)
                        stt(Dh[:, :], cur[:, 16:1040], 1.0 / w, VP[:, 16:1040], ALU.mult, ALU.subtract,
                            [curk] + vk, [("Dp", hf)])
                        if hh == 0:
                            t, tk = tmpw()
                            tt("dve", t[:, 0:16], cur[:, 16:32], ctab[:, 16 * g:16 * g + 16], ALU.mult, [curk, "ctab"], [tk])
                            tt("dve", Dh[:, 0:16], t[:, 0:16], VP[:, 16:32], ALU.subtract, [tk] + vk + [("Dp", hf)],
                               [("Dp", hf)])

                    def deferred(slot, buf, g=g, gp=gp):
                        for e_ in range(2):
                            pp = pspair()
                            for n in range(2):
                                mm(pp, n, [(WS[slot][:, 2048 + h2 * 256 + e_ * 128:2048 + h2 * 256 + (e_ + 1) * 128],
                                            Dv(h2)[:, tl(n)]) for h2 in range(2)],
                                   [("W", slot), ("Dp", 0), ("Dp", 1)])
                            t, tk = tmpw()
                            act(t, psw(pp), AF.Identity, pk(pp) + ["cvec", "DERpbs%d" % j], [tk],
                                scale=cv(cb0 + E_POOLS + 2 * g + e_), bias=dc(48 + 8 * j + 2 * g + e_))
                            tt("dve", YBv(2 * g + e_)[:, :], t, SGBv(gp, e_), ALU.mult,
                               [tk, ("SGB", gp, e_)], [("YB", 2 * g + e_)])
                    steps.append(dict(dma=dma, head=head, tail=tail, deferred=(deferred if hf == 1 else None)))

            a1box = {}
            for ca in range(8):
                def dma(slot, ca=ca):
                    load_w(slot, [(0, 2048, wie[j, 2 * ca:2 * ca + 2], "pair")])

                def aux(buf, ca=ca):
                    col0 = cb0 + E_CONVW + ca * 32

                    def fn(e):
                        in0 = I4[:].rearrange("p (o c) -> p o c", o=1).to_broadcast([128, 32, 32])
                        in1 = cvec[:, col0:col0 + 32].rearrange("p (q o) -> p q o", o=1).to_broadcast([128, 32, 32])
                        return e.tensor_tensor(out=LT[buf][:, :, :], in0=in0, in1=in1, op=ALU.mult)
                    P.op("pool", fn, reads=["I4", "cvec"], writes=[("LT", buf)])

                def head(slot, buf, ca=ca):
                    if ca <= 1:
                        P.fence(("REG", "R"))
                    if ca == 0:
                        mset("pool", ST[:, 0, :], 0.0, ["ACC1"])
                        mset("pool", ST[:, 1, :], 0.0, ["ACC2"])
                    ub = UB[buf]

                    def ub_pad(b_, c_):
                        if hh == 0:
                            mset("pool", UB[b_][:, 0:32], 0.0, [("UB", b_, "pad")])
                        else:
                            cp("pool", UB[b_][:, 0:32], UHALO[:, c_, :], [("UHALO", c_)], [("UB", b_, "pad")])
                    if ca == 0:
                        ub_pad(buf, 0)
                    if ca + 1 < 8:
                        ub_pad(1 - buf, ca + 1)
                    pg, pv = pspair(), pspair()
                    inproj(pg, slot, 1024)
                    sg, sgk = tmpw()
                    act(sg, psw(pg), AF.Sigmoid, pk(pg), [sgk])
                    inproj(pv, slot, 0)
                    tt("dve", ub[:, 32:32 + TH], psw(pv), sg, ALU.mult, pk(pv) + [sgk], [("UB", buf, "d")])

                def conv(buf, ca):
                    pc = pspair()

                    def fn(e):
                        r = None
                        for n in range(2):
                            for g in range(8):
                                for jb in range(4):
                                    r = e.matmul(PSALL[32 * jb:32 * jb + 32, 2 * pc + n, :], LT[buf][:, jb * 8 + g, :],
                                                 SST[:, jb, 2 + 4 * g + n * NT:2 + 4 * g + (n + 1) * NT],
                                                 start=(g == 0), stop=(g == 7), tile_position=(0, 32 * jb))
                        return r
                    P.op("pe", fn, reads=[("LT", buf), ("SST", "a"), ("SST", "b")], writes=pk(pc))
                    ts("dve", Uv(ca), psw(pc), cv(cb0 + E_CONVB + ca), None, ALU.add, None, pk(pc) + ["cvec"],
                       [("U", ca), ("YA", ca)])
                    us, usk = tmpw()
                    act(us, Uv(ca), AF.Square, [("U", ca)], [usk])
                    tt("dve", ST[:, 0, :], ST[:, 0, :], Uv(ca), ALU.add, ["ACC1", ("U", ca)], ["ACC1"])
                    tt("dve", ST[:, 1, :], ST[:, 1, :], us, ALU.add, ["ACC2", usk], ["ACC2"])

                def tail(slot, buf, ca=ca):
                    ub = UB[buf]
                    ubk = [("UB", buf, "pad"), ("UB", buf, "d"), ("UB", buf, "tail")]

                    def fs(jbs):
                        return lambda e: [e.dma_start(out=SST[32 * kk:32 * kk + 32, jb, :], in_=ub[32 * jb:32 * jb + 32, kk:kk + 1056])
                                          for jb in jbs for kk in range(4)]
                    P.op("sp", fs((0, 1, 2)), reads=ubk, writes=[("SST", "a")], dsem=s_s, n=12)
                    P.op("pool", fs((3,)), reads=ubk, writes=[("SST", "b")], dsem=s_s2, n=4)
                    if hh == 0:
                        cp("pool", UHALO[:, ca, :], ub[:, 1024:1056], [("UB", buf, "d")], [("UHALO", ca)])
                    if ca == 7:
                        conv(buf, ca)
                        p1, p2 = pspair(), pspair()
                        for n in range(2):
                            P.op("pe", lambda e, n=n: e.matmul(psb(p1, n), onesD[:], ST[:, 0, tl(n)], start=True, stop=True),
                                 reads=["onesD", "ACC1"], writes=[("ps", 2 * p1 + n)])
                            P.op("pe", lambda e, n=n: e.matmul(psb(p2, n), onesD[:], ST[:, 1, tl(n)], start=True, stop=True),
                                 reads=["onesD", "ACC2"], writes=[("ps", 2 * p2 + n)])
                        cp("dve", ST[:, 0, :], psw(p1), pk(p1), ["ACC1", "MU"])
                        t, tk = tmpw()
                        tt("dve", t, ST[:, 0, :], ST[:, 0, :], ALU.mult, ["MU"], [tk])
                        tt("dve", t, psw(p2), t, ALU.subtract, pk(p2) + [tk], [tk])
                        act(t, t, AF.Ln, [tk, "DERe"], [tk], bias=dc(113))
                        act(ST[:, 1, :], t, AF.Exp, [tk], ["ACC2", "RSTD"], scale=-0.5)
                steps.append(dict(dma=dma, aux=aux, head=head, tail=tail, aux_late=True,
                                  deferred=((lambda slot, buf, ca=ca: conv(buf, ca)) if ca < 7 else None)))

            a2box = {}
            for ca in range(8):
                def dma(slot, ca=ca):
                    load_w(slot, [(0, 1024, wie[j, 32 + ca])])

                def head(slot, buf, ca=ca, box=None):
                    pg = pspair()
                    inproj(pg, slot, 0)
                    sg, sgk = tmpw()
                    act(sg, psw(pg), AF.Silu, pk(pg), [sgk])
                    t1, t1k = tmpw()
                    tt("dve", t1, Uv(ca), ST[:, 0, :], ALU.subtract, [("U", ca), "MU"], [t1k])
                    tt("dve", t1, t1, ST[:, 1, :], ALU.mult, [t1k, "RSTD"], [t1k])
                    act(t1, t1, AF.Silu, [t1k, "cvec"], [t1k], scale=cv(cb0 + E_LNG + ca), bias=cv(cb0 + E_LNB + ca))
                    a2box[ca] = (sg, sgk, t1, t1k)

                def deferred(slot, buf, ca=ca):
                    sg, sgk, t1, t1k = a2box[ca]
                    tt("dve", YAv(ca)[:, :], t1, sg, ALU.mult, [t1k, sgk], [("YA", ca), ("U", ca)])
                steps.append(dict(dma=dma, head=head, deferred=deferred))

            obox = {}
            for mo in range(8):
                def dma(slot, mo=mo):
                    load_w(slot, [(0, 2048, woe[j, mo])])

                def head(slot, buf, mo=mo):
                    if mo == 0:
                        P.fence(("REG", "E"))
                    po = pspair()
                    obox[mo] = po
                    for n in range(2):
                        pairs = [(WS[slot][:, (8 + kc) * 128:(9 + kc) * 128], YBv(kc)[:, tl(n)]) for kc in range(8)]
                        pairs += [(WS[slot][:, kc * 128:(kc + 1) * 128], YAv(kc)[:, tl(n)]) for kc in range(6)]
                        mm_part(po, n, pairs, [("W", slot)] + [("YA", kc) for kc in range(6)] + [("YB", kc) for kc in range(8)],
                                first=True, last=False)

                def deferred2(slot, buf, mo=mo):
                    po = obox[mo]
                    for n in range(2):
                        pairs = [(WS[slot][:, kc * 128:(kc + 1) * 128], YAv(kc)[:, tl(n)]) for kc in (6, 7)]
                        mm_part(po, n, pairs, [("W", slot), ("YA", 6), ("YA", 7)], first=False, last=True)
                    hs = H[:, mo, hh * TH:(hh + 1) * TH]
                    tt("dve", hs, psw(po), hs, ALU.add, pk(po) + [("H", mo, hh)], [("H", mo, hh)])
                steps.append(dict(dma=dma, head=head, deferred=deferred2, flush=(mo == 0), ofirst=(mo == 0)))

        def odd_half(j, hh):
            ob = OB[j]

            def pre(slot, buf):
                P.fence(("REG", "R"))
                P.fence(("REG", "E"))
            steps.append(dict(head=pre, norm=(lambda: norm_half(hh, ob + O_NORM)), flush=True))

            for m in range(NHEAD):
                def dma(slot, m=m):
                    load_w(slot, [(0, 2048, wio[j, 2 * m:2 * m + 2], "pair"), (2048, 256, wgt[j, m])])

                def aux(buf, m=m):
                    build_dg(buf, ob + O_CCW + 4 * m, 4)

                def xr_pad(b_, m_):
                    xr_ = XRv(b_)
                    if hh == 0:
                        mset("pool", xr_[:, 0:8], 0.0, [("XR", b_, "pad")])
                    else:
                        cp("pool", xr_[:, 4:8], XHALO[:, m_, :], [("XHALO", m_)], [("XR", b_, "pad")])

                def head(slot, buf, m=m):
                    xr = XRv(buf)
                    if m == 0:
                        xr_pad(buf, 0)
                    if m + 1 < NHEAD:
                        xr_pad(1 - buf, m + 1)
                    pg = pspair()
                    inproj(pg, slot, 1024)
                    SG = SGv(m % 4)
                    act(SG, psw(pg), AF.Tanh, pk(pg), [("SG", m % 4)], scale=0.5)
                    stt(SG, SG, 1.0, psw(pg), ALU.add, ALU.mult, [("SG", m % 4)] + pk(pg), [("SG", m % 4)])
                    px = pspair()
                    inproj(px, slot, 0)
                    cp("dve", xr[:, 8:8 + TH], psw(px), pk(px), [("XR", buf, "d")])
                    if hh == 0:
                        cp("dve", XHALO[:, m, :], xr[:, 1028:1032], [("XR", buf, "d")], [("XHALO", m)])

                def tail(slot, buf, m=m):
                    xr = XRv(buf)
                    pc = pspair()
                    for n in range(2):
                        mm(pc, n, [(DG[buf][:, k, :], xr[:, 5 + n * NT + k:5 + n * NT + k + NT]) for k in range(4)],
                           [("DG", buf), ("XR", buf, "pad"), ("XR", buf, "d")])
                    ts("dve", TXv(buf), psw(pc), cv(ob + O_CCB + m), None, ALU.add, None, pk(pc) + ["cvec"], [("TX", buf)])
                    ts("dve", XCBv(buf), psw(pc), cv(ob + O_CCB + m), None, ALU.add, None, pk(pc) + ["cvec"], [("XCB", buf)])

                def deferred(slot, buf, m=m):
                    TR, TRk = TMP[:, 0, :], ("TMP", 0)
                    ti = 1 if buf == 0 else 3
                    TI, TIk = TMP[:, ti, :], ("TMP", ti)
                    TE, TEk = TMP[:, 2, :], ("TMP", 2)
                    pr, pi = pspair(), pspair()
                    for n in range(2):
                        mm(pr, n, [(WS[slot][:, 2048:2176], XCBv(buf)[:, tl(n)])], [("W", slot), ("XCB", buf)])
                        mm(pi, n, [(WS[slot][:, 2176:2304], XCBv(buf)[:, tl(n)])], [("W", slot), ("XCB", buf)])
                    act(TR, psw(pr), AF.Tanh, pk(pr) + ["DERhb"], [TRk], scale=0.5, bias=dc(128 + 12 * j + m))
                    act(TI, psw(pi), AF.Tanh, pk(pi) + ["DERhb"], [TIk], scale=0.5, bias=dc(152 + 12 * j + m))
                    act(TE, TR, AF.Exp, [TRk, "DERsp"], [TEk], scale=dc(12 * j + m), bias=dc(12 * j + m))
                    act(Av(buf), TR, AF.Exp, [TRk, "DERsp"], [("A", buf)], scale=dc(24 + 12 * j + m), bias=dc(24 + 12 * j + m))
                    act(TE, TE, AF.Relu, [TEk, "DERe"], [TEk], scale=-0.0625, bias=dc(115))
                    act(TE, TE, AF.Sqrt, [TEk], [TEk])
                    act(DER[:, 120:121], DER[:, 114:115], AF.Tanh, ["DERe"], [("DUM",)])
                    stt(TI, TI, 1.0, TXv(buf), ALU.add, ALU.mult, [TIk, ("TX", buf)], [TIk])
                    tt("dve", TI, TI, TE, ALU.mult, [TIk, TEk], [TIk])

                def deferred2(slot, buf, m=m):
                    ti = 1 if buf == 0 else 3
                    TI, TIk = TMP[:, ti, :], ("TMP", ti)
                    init = 0.0 if hh == 0 else CARRY[:, m:m + 1]
                    P.op("dve", lambda e: e.tensor_tensor_scan(out=HSv, data0=Av(buf), data1=TI, initial=init,
                                                               op0=ALU.mult, op1=ALU.add),
                         reads=[("A", buf), TIk, ("CARRY", m)], writes=[("HS",)])
                    if hh == 0:
                        cp("dve", CARRY[:, m:m + 1], HSv[:, TH - 1:TH], [("HS",)], [("CARRY", m)])
                    tt("dve", Yv(m)[:, :], SGv(m % 4), HSv, ALU.mult, [("SG", m % 4), ("HS",)], [("Y", m)])
                steps.append(dict(dma=dma, aux=aux, head=head, tail=tail, deferred=deferred, deferred2=deferred2))

            oobox = {}
            for mo in range(8):
                def dma(slot, mo=mo):
                    load_w(slot, [(0, 1536, woo[j, mo])])

                def head(slot, buf, mo=mo):
                    if mo == 0:
                        P.fence(("REG", "E"))
                    po = pspair()
                    oobox[mo] = po
                    for n in range(2):
                        mm_part(po, n, [(WS[slot][:, kc * 128:(kc + 1) * 128], Yv(kc)[:, tl(n)]) for kc in range(NHEAD - 2)],
                                [("W", slot)] + [("Y", kc) for kc in range(NHEAD - 2)], first=True, last=False)

                def deferred2(slot, buf, mo=mo):
                    po = oobox[mo]
                    for n in range(2):
                        mm_part(po, n, [(WS[slot][:, kc * 128:(kc + 1) * 128], Yv(kc)[:, tl(n)]) for kc in (NHEAD - 2, NHEAD - 1)],
                                [("W", slot), ("Y", NHEAD - 2), ("Y", NHEAD - 1)], first=False, last=True)
                    hs = H[:, mo, hh * TH:(hh + 1) * TH]
                    tt("dve", hs, psw(po), hs, ALU.add, pk(po) + [("H", mo, hh)], [("H", mo, hh)])
                steps.append(dict(dma=dma, head=head, deferred=deferred2, flush=(mo == 0), ofirst=(mo == 0)))

        def final_norm_half(hh):
            def fn(slot, buf):
                rms_stats(hh)
                for kc in range(KC):
                    i = kc % 2
                    stg = ST[:, i, :]
                    stt(stg, H[:, kc, hh * TH:(hh + 1) * TH], cv(FN_BASE + kc), ST[:, 2, :], ALU.mult, ALU.mult,
                        [("H", kc, hh), "RS", "cvec"], [("ST", i)])
                    P.op("sp", lambda e, kc=kc, stg=stg: e.dma_start(out=outd[kc * 128:(kc + 1) * 128, hh * TH:(hh + 1) * TH], in_=stg),
                         reads=[("ST", i)], dsem=s_o[i])
            steps.append(dict(head=fn, flush=True))

        def store_h_half(hh):
            def fn(slot, buf):
                for kc in range(KC):
                    P.op("sp", lambda e, kc=kc: e.dma_start(out=outd[kc * 128:(kc + 1) * 128, hh * TH:(hh + 1) * TH],
                                                            in_=H[:, kc, hh * TH:(hh + 1) * TH]),
                         reads=[("H", kc, hh)], dsem=s_o[kc % 2])
            steps.append(dict(head=fn, flush=True))

        for L in layers:
            for hh in range(2):
                if L % 2 == 0:
                    even_half(L // 2, hh)
                else:
                    odd_half(L // 2, hh)
        for hh in range(2):
            if do_final_norm:
                final_norm_half(hh)
            else:
                store_h_half(hh)

        wl = [i for i, s_ in enumerate(steps) if s_.get("dma") is not None]
        pos = {i: q for q, i in enumerate(wl)}
        sb_of = {i: (q % 3, q % 2) for q, i in enumerate(wl)}
        issued = {"dma": 0, "aux": 0}

        def issue(kind, upto):
            while issued[kind] <= upto and issued[kind] < len(wl):
                i = wl[issued[kind]]
                f = steps[i].get(kind)
                if f is not None:
                    f(sb_of[i][0] if kind == "dma" else sb_of[i][1])
                issued[kind] += 1

        issue("dma", 1)
        issue("aux", 0)
        pend = {"A": None, "B0": None, "B1": None}

        def run(k):
            if pend[k] is not None:
                pend[k]()
                pend[k] = None

        for i, s_ in enumerate(steps):
            slot, buf = sb_of.get(i, (None, None))
            if s_.get("flush"):
                run("A")
                run("B0")
                run("B1")
            s_["head"](slot, buf)
            if s_.get("norm") is not None and not s_.get("norm_done"):
                s_["norm"]()
                s_["norm_done"] = True
            if s_.get("ofirst"):
                for s2 in steps[i + 1:]:
                    if s2.get("norm") is not None:
                        if not s2.get("norm_done"):
                            s2["norm"]()
                            s2["norm_done"] = True
                        break
            run("B0")
            run("A")
            if i in pos:
                issue("dma", pos[i] + 2)
                if not s_.get("aux_late"):
                    issue("aux", pos[i] + 1)
            if s_.get("tail") is not None:
                s_["tail"](slot, buf)
            if i in pos and s_.get("aux_late"):
                issue("aux", pos[i] + 1)
            pend["B0"], pend["B1"] = pend["B1"], None
            if s_.get("deferred") is not None:
                pend["A"] = (lambda f=s_["deferred"], slot=slot, buf=buf: f(slot, buf))
            if s_.get("deferred2") is not None:
                pend["B1"] = (lambda f=s_["deferred2"], slot=slot, buf=buf: f(slot, buf))
        run("A")
        run("B0")
        run("B1")

        P.final_wait("sp", s_o)
        with nc.Block() as block:
            P.replay(block)
    return nc


def _chunk_cols(v):
    v = np.asarray(v, np.float32)
    return np.ascontiguousarray(v.reshape(-1, 128).T)


def _prep_shared(inp):
    f = lambda a: np.asarray(a, np.float32)
    cvec = np.zeros((128, NCV), np.float32)
    for j in range(2):
        b = EB[j]
        cvec[:, b + E_NORM:b + E_NORM + 8] = _chunk_cols(f(inp["norm_even"])[j])
        cvec[:, b + E_CONVB:b + E_CONVB + 8] = _chunk_cols(f(inp["conv_a_b"])[j])
        cvec[:, b + E_LNG:b + E_LNG + 8] = _chunk_cols(f(inp["ln_a_g"])[j])
        cvec[:, b + E_LNB:b + E_LNB + 8] = _chunk_cols(f(inp["ln_a_b"])[j])
        cvec[:, b + E_POOLB:b + E_POOLB + 8] = _chunk_cols(f(inp["pool_b"])[j].reshape(-1))
        cvec[:, b + E_POOLS:b + E_POOLS + 8] = _chunk_cols(f(inp["pool_scale"])[j])
        cw = np.concatenate([f(inp["conv_a_w"])[j], np.zeros((1, 1024), np.float32)], axis=0)
        cvec[:, b + E_CONVW:b + E_CONVW + 256] = cw.reshape(8, 4, 8, 4, 32).transpose(1, 4, 2, 3, 0).reshape(128, 256)
        o = OB[j]
        cvec[:, o + O_NORM:o + O_NORM + 8] = _chunk_cols(f(inp["norm_odd"])[j])
        cvec[:, o + O_CCB:o + O_CCB + 12] = _chunk_cols(f(inp["conv_c_b"])[j])
        cvec[:, o + O_BRG:o + O_BRG + 12] = _chunk_cols(f(inp["b_rg"])[j])
        cvec[:, o + O_BIG:o + O_BIG + 12] = _chunk_cols(f(inp["b_ig"])[j])
        cvec[:, o + O_LAM:o + O_LAM + 12] = _chunk_cols(f(inp["lru_lambda"])[j])
        ccw = f(inp["conv_c_w"])[j]
        cvec[:, o + O_CCW:o + O_CCW + 48] = ccw.reshape(4, 12, 128).transpose(2, 1, 0).reshape(128, 48)
    cvec[:, FN_BASE:FN_BASE + 8] = _chunk_cols(f(inp["final_norm"]))
    ctab = np.zeros((128, 64), np.float32)
    for g, w in enumerate(POOL_WINDOWS):
        ctab[:, 16 * g:16 * g + 16] = 1.0 / np.minimum(np.arange(1, 17), w).astype(np.float32)
    wie = f(inp["w_in_even"]).reshape(2, 8, 128, 40, 128).transpose(0, 3, 2, 1, 4).reshape(2, 40, 128, 1024)
    order_e = [c for ca in range(8) for c in (ca, 8 + ca)] + [c for cb in range(8) for c in (24 + cb, 32 + cb)] + [16 + ca for ca in range(8)]
    wie = np.ascontiguousarray(wie[:, order_e])
    woe = np.ascontiguousarray(f(inp["w_out_even"]).reshape(2, 16, 128, 8, 128).transpose(0, 3, 2, 1, 4).reshape(2, 8, 128, 2048))
    wio = f(inp["w_in_odd"]).reshape(2, 8, 128, 24, 128).transpose(0, 3, 2, 1, 4).reshape(2, 24, 128, 1024)
    order_o = [c for m in range(12) for c in (m, 12 + m)]
    wio = np.ascontiguousarray(wio[:, order_o])
    woo = np.ascontiguousarray(f(inp["w_out_odd"]).reshape(2, 12, 128, 8, 128).transpose(0, 3, 2, 1, 4).reshape(2, 8, 128, 1536))
    pw = np.ascontiguousarray(f(inp["pool_w"]).reshape(2, 4, 2, 128, 2, 128).transpose(0, 1, 3, 2, 4, 5).reshape(2, 4, 128, 512))
    wgt = np.ascontiguousarray(np.concatenate([f(inp["w_rg"]), f(inp["w_ig"])], axis=-1))
    return {"cvec": cvec, "ctab": ctab, "wie": wie, "woe": woe, "wio": wio, "woo": woo, "pw": pw, "wgt": wgt}


_PROGS = {}


def _get_prog(layers, fin):
    key = (tuple(layers), fin)
    if key not in _PROGS:
        _PROGS[key] = build_program(list(layers), fin)
    return _PROGS[key]


LAUNCH_PLAN = [((0, 1, 2, 3), True)]


def kernel(**inputs):
    shared = _prep_shared(inputs)
    x = np.asarray(inputs["x"], np.float32)
    n = x.shape[0]
    cur = [np.ascontiguousarray(x[b].T) for b in range(n)]
    for (layers, fin) in LAUNCH_PLAN:
        nc = _get_prog(layers, fin)
        in_maps = [dict(shared, xin=cur[b]) for b in range(n)]
        res = run_bass_kernel_spmd(nc, in_maps, core_ids=list(range(n)))
        cur = [np.asarray(res.results[b]["out"], np.float32) for b in range(n)]
    return np.stack([c.T for c in cur], axis=0).astype(np.float32)
```

```python
import numpy as np
from contextlib import ExitStack
import concourse.bass as bass
import concourse.mybir as mybir
from concourse.bass_utils import run_bass_kernel_spmd

F32 = mybir.dt.float32
BF16 = mybir.dt.bfloat16
F32R = mybir.dt.float32r
I32 = mybir.dt.int32
AF = mybir.ActivationFunctionType
ALU = mybir.AluOpType

D = 1024
T = 2048
TH = 1024
NT = 512
KC = 8
EPS_RMS = 1e-6
EPS_LN = 1e-5
CONV_K = 31
POOL_WINDOWS = (2, 4, 8, 16)
NHEAD = 12
DVE_TAPS = 8

E_NORM, E_CONVB, E_LNG, E_LNB, E_POOLB, E_POOLS, E_CONVW = 0, 8, 16, 24, 32, 40, 48
E_SIZE = 48 + 8 * 32
O_NORM, O_CCB, O_BRG, O_BIG, O_LAM, O_CCW = 0, 8, 20, 32, 44, 56
O_SIZE = 56 + 48
EB = [0, E_SIZE]
OB = [2 * E_SIZE, 2 * E_SIZE + O_SIZE]
FN_BASE = 2 * E_SIZE + 2 * O_SIZE
NCV = FN_BASE + 8


class Sem:
    def __init__(self, handle, inc):
        self.h = handle
        self.inc = inc
        self.count = 0


class Prog:
    ENGS = ("pe", "act", "dve", "pool", "sp")

    def __init__(self, nc, stack):
        self.nc = nc
        self.stack = stack
        self.lists = {e: [] for e in self.ENGS}
        self.esem = {e: Sem(stack.enter_context(nc.semaphore("s_" + e)), 1) for e in self.ENGS}
        self.waited = {e: {} for e in self.ENGS}
        self.last_write = {}
        self.readers = {}
        self.region_of = {}

    def new_dma_sem(self, name):
        return Sem(self.stack.enter_context(self.nc.semaphore("d_" + name)), 16)

    def _deps(self, reads, writes, own=None):
        waits = {}

        def need(s, v):
            if waits.get(s, 0) < v:
                waits[s] = v

        reads = list(reads)
        for k in list(reads) + list(writes):
            rk = self.region_of.get(k[0]) if isinstance(k, tuple) else None
            if rk is not None and rk not in reads:
                reads.append(rk)
        for k in reads:
            for (s, v) in self.last_write.get(k, ()):
                need(s, v)
            if isinstance(k, tuple) and k[0] == "ps":
                for (s, v) in self.readers.get(k, {}).items():
                    if s is not own:
                        need(s, v)
        for k in writes:
            for (s, v) in self.last_write.get(k, ()):
                need(s, v)
            for (s, v) in self.readers.get(k, {}).items():
                need(s, v)
        return reads, waits

    def op(self, eng, fn, reads=(), writes=(), dsem=None, n=1):
        done = dsem if dsem is not None else self.esem[eng]
        reads, waits = self._deps(reads, writes, own=done)
        wl = []
        for s, v in waits.items():
            if eng == "pe" and s is self.esem["pe"]:
                continue
            if self.waited[eng].get(s, 0) >= v:
                continue
            self.waited[eng][s] = v
            wl.append((s.h, v))
        done.count += done.inc * n
        val = done.count
        sh, inc = done.h, done.inc

        def emit(e, wl=wl, fn=fn, sh=sh, inc=inc, n=n):
            for (h, v) in wl:
                e.wait_ge(h, v)
            r = fn(e)
            if isinstance(r, (list, tuple)):
                assert len(r) == n
                for x in r:
                    x.then_inc(sh, inc)
            else:
                assert n == 1
                r.then_inc(sh, inc)

        self.lists[eng].append(emit)
        for k in writes:
            self.last_write[k] = [(done, val)]
            self.readers[k] = {}
        for k in reads:
            self.readers.setdefault(k, {})[done] = val
        return val

    def fence(self, key):
        cur = {}
        for (s, v) in self.last_write.get(key, ()):
            cur[s] = max(cur.get(s, 0), v)
        for (s, v) in self.readers.get(key, {}).items():
            cur[s] = max(cur.get(s, 0), v)
        self.last_write[key] = list(cur.items())
        self.readers[key] = {}

    def final_wait(self, eng, sems):
        wl = [(s.h, s.count) for s in sems if s.count > 0]

        def emit(e, wl=wl):
            for (h, v) in wl:
                e.wait_ge(h, v)

        self.lists[eng].append(emit)

    def replay(self, block):
        L = self.lists

        @block.tensor
        def _(e):
            for f in L["pe"]:
                f(e)

        @block.scalar
        def _(e):
            for f in L["act"]:
                f(e)

        @block.vector
        def _(e):
            for f in L["dve"]:
                f(e)

        @block.gpsimd
        def _(e):
            for f in L["pool"]:
                f(e)

        @block.sync
        def _(e):
            for f in L["sp"]:
                f(e)


def build_program(layers, do_final_norm):
    nc = bass.Bass("TRN2", target_bir_lowering=False)
    dr = lambda name, shape, kind="ExternalInput": nc.dram_tensor(name, shape, F32, kind=kind).ap()
    xin = dr("xin", [D, T])
    cvec_d = dr("cvec", [128, NCV])
    ctab_d = dr("ctab", [128, 64])
    wie = dr("wie", [2, 40, 128, 1024])
    woe = dr("woe", [2, 8, 128, 2048])
    wio = dr("wio", [2, 24, 128, 1024])
    woo = dr("woo", [2, 8, 128, 1536])
    pwd = dr("pw", [2, 4, 128, 512])
    wgt = dr("wgt", [2, 12, 128, 256])
    outd = dr("out", [D, T], kind="ExternalOutput")

    with ExitStack() as st:
        P = Prog(nc, st)
        sb = lambda name, shape, dt=F32: st.enter_context(nc.sbuf_tensor(name, shape, dt))
        H = sb("H", [128, KC, T])
        HN = sb("HN", [128, KC, TH], BF16)
        R = sb("R", [128, 13384])
        ST = sb("ST", [128, 3, TH])
        WS = [sb("WS%d" % i, [128, 2560], BF16) for i in range(3)]
        CV = sb("CV", [128, 2 * CONV_K * 128], BF16)
        DG = [CV[:, i * CONV_K * 128:(i + 1) * CONV_K * 128].rearrange("p (k j) -> p k j", k=CONV_K) for i in range(2)]
        SST = CV[:, 0:4 * 1056].rearrange("p (j x) -> p j x", j=4)
        LT = [CV[:, 4224 + i * 1024:4224 + (i + 1) * 1024].rearrange("p (q c) -> p q c", q=32) for i in range(2)]
        UB = [sb("UB%d" % i, [128, 40 + TH], BF16) for i in range(2)]
        I4 = sb("I4", [128, 32], BF16)
        TMP = sb("TMP", [128, 4, TH])
        cvec = sb("cvec_s", [128, NCV])
        ctab = sb("ctab_s", [128, 64])
        iot = sb("iot", [128, 128], I32)
        ident = sb("ident", [128, 128], BF16)
        onesD = sb("onesD", [128, 128])
        onesR = sb("onesR", [128, 128], F32R)
        SQR = sb("SQR", [128, TH], F32R)
        UHALO = sb("UHALO", [128, 8, 32], BF16)
        VHALO = sb("VHALO", [128, 8, 16])
        XHALO = sb("XHALO", [128, NHEAD, 4], BF16)
        CARRY = sb("CARRY", [128, NHEAD])
        DER = sb("DER", [128, 192])
        PSALL = st.enter_context(nc.psum_tensor("psall", [128, 8, NT], F32))

        Uv = lambda ca: R[:, ca * TH:(ca + 1) * TH]
        YAv = lambda ca: R[:, ca * TH:ca * TH + TH // 2].bitcast(BF16)
        YBv = lambda cb: R[:, 8256 + cb * 512:8256 + (cb + 1) * 512].bitcast(BF16)
        VPv = lambda hf: R[:, hf * 1040:(hf + 1) * 1040]
        SAv = R[:, 2080:3120]
        SBv = R[:, 3120:4160]
        SGBv = lambda gp, hf: R[:, 4160 + (2 * gp + hf) * TH:4160 + (2 * gp + hf + 1) * TH]
        Dv = lambda hf: R[:, 12352 + hf * 512:12352 + (hf + 1) * 512].bitcast(BF16)
        Yv = lambda m: R[:, m * 512:(m + 1) * 512].bitcast(BF16)
        Av = lambda b: R[:, 6144 + b * TH:6144 + (b + 1) * TH]
        HSv = R[:, 8192:9216]
        SGv = lambda b4: R[:, 9216 + b4 * TH:9216 + (b4 + 1) * TH]
        XRv = lambda b: DG[b][:, 20:29, :].rearrange("p a b -> p (a b)")[:, 0:1032]
        TXv = lambda b: DG[b][:, 4:20, :].rearrange("p a b -> p (a b)").bitcast(F32)
        XCBv = lambda b: UB[b][:, 0:TH]
        for nm in ("U", "YA", "YB", "VP", "SA", "SB", "SGB", "Dp", "Y", "A", "HS", "SG"):
            P.region_of[nm] = ("REG", "R")
        for nm in ("DG", "TX", "UB", "XCB", "XR", "SST", "LT"):
            P.region_of[nm] = ("REG", "E")

        tl = lambda n: slice(n * NT, (n + 1) * NT)
        cv = lambda col: cvec[:, col:col + 1]
        dc = lambda col: DER[:, col:col + 1]
        psb = lambda p, n: PSALL[:, 2 * p + n, :]
        psw = lambda p: PSALL[:, 2 * p:2 * p + 2, :].rearrange("p a b -> p (a b)")
        pk = lambda p: [("ps", 2 * p), ("ps", 2 * p + 1)]

        state = {"pp": 0, "tmp": 0}

        def pspair():
            p = state["pp"]
            state["pp"] = (p + 1) % 4
            return p

        def tmpw():
            i = state["tmp"]
            state["tmp"] = (i + 1) % 4
            return TMP[:, i, :], ("TMP", i)

        def mm(p, n, pairs, reads):
            def fn(e):
                k = len(pairs)
                r = None
                for i, (l, rh) in enumerate(pairs):
                    r = e.matmul(psb(p, n), l, rh, start=(i == 0), stop=(i == k - 1))
                return r
            P.op("pe", fn, reads=reads, writes=[("ps", 2 * p + n)])

        def mm_part(p, n, pairs, reads, first, last):
            def fn(e):
                k = len(pairs)
                r = None
                for i, (l, rh) in enumerate(pairs):
                    r = e.matmul(psb(p, n), l, rh, start=(first and i == 0), stop=(last and i == k - 1))
                return r
            P.op("pe", fn, reads=reads, writes=[("ps", 2 * p + n)])

        def act(out, in_, func, reads, writes, bias=None, scale=None):
            kw = {}
            if bias is not None:
                kw["bias"] = bias
            if scale is not None:
                kw["scale"] = scale
            P.op("act", lambda e: e.activation(out=out, in_=in_, func=func, **kw), reads=reads, writes=writes)

        def tt(eng, out, in0, in1, op, reads, writes):
            P.op(eng, lambda e: e.tensor_tensor(out=out, in0=in0, in1=in1, op=op), reads=reads, writes=writes)

        def ts(eng, out, in0, s1, s2, op0, op1, reads, writes):
            if op1 is None:
                P.op(eng, lambda e: e.tensor_scalar(out=out, in0=in0, scalar1=s1, scalar2=None, op0=op0),
                     reads=reads, writes=writes)
            else:
                P.op(eng, lambda e: e.tensor_scalar(out=out, in0=in0, scalar1=s1, scalar2=s2, op0=op0, op1=op1),
                     reads=reads, writes=writes)

        def stt(out, in0, scalar, in1, op0, op1, reads, writes):
            P.op("dve", lambda e: e.scalar_tensor_tensor(out=out, in0=in0, scalar=scalar, in1=in1, op0=op0, op1=op1),
                 reads=reads, writes=writes)

        def cp(eng, out, in_, reads, writes):
            P.op(eng, lambda e: e.tensor_copy(out=out, in_=in_), reads=reads, writes=writes)

        def mset(eng, ap, val, writes):
            P.op(eng, lambda e: e.memset(ap, val), writes=writes)

        s_x = P.new_dma_sem("x")
        s_c = P.new_dma_sem("c")
        s_w = [P.new_dma_sem("w%d" % i) for i in range(3)]
        s_o = [P.new_dma_sem("o%d" % i) for i in range(2)]
        s_s = P.new_dma_sem("sst")
        s_s2 = P.new_dma_sem("sst2")
        P.op("sp", lambda e: [e.dma_start(out=cvec[:], in_=cvec_d[:, :]), e.dma_start(out=ctab[:], in_=ctab_d[:, :])],
             writes=["cvec", "ctab"], dsem=s_c, n=2)
        s_x2 = P.new_dma_sem("x2")
        s_xk = [P.new_dma_sem("xk%d" % kc) for kc in range(KC)]
        for kc in range(KC):
            P.op("sp", lambda e, kc=kc: e.dma_start(out=H[:, kc, 0:TH], in_=xin[kc * 128:(kc + 1) * 128, 0:TH]),
                 writes=[("H", kc, 0)], dsem=s_xk[kc])
        P.op("sp", lambda e: [e.dma_start(out=H[:, kc, TH:T], in_=xin[kc * 128:(kc + 1) * 128, TH:T]) for kc in range(KC)],
             writes=[("H", kc, 1) for kc in range(KC)], dsem=s_x2, n=KC)

        P.op("pool", lambda e: e.iota(iot[:], pattern=[[1, 128]], base=0, channel_multiplier=-1), writes=["iot"])
        ts("dve", ident[:], iot[:], 0.0, None, ALU.is_equal, None, ["iot"], ["ident"])
        mset("pool", onesD[:], 1.0 / D, ["onesD"])
        for q4 in range(4):
            cp("dve", I4[32 * q4:32 * q4 + 32, :], ident[32 * q4:32 * q4 + 32, 32 * q4:32 * q4 + 32], ["ident"], [("I4", q4)])
        P.last_write["I4"] = sum([P.last_write[("I4", q4)] for q4 in range(4)], [])
        for b_ in range(2):
            mset("pool", UB[b_][:, 32 + TH:40 + TH], 0.0, [("UB", b_, "tail")])
        cp("dve", onesR[:], onesD[:], ["onesD"], ["onesR"])
        mset("pool", CARRY[:], 0.0, ["CARRY"])
        mset("pool", DER[:, 112:113], EPS_RMS, ["DERe0"])
        mset("pool", DER[:, 113:114], EPS_LN, ["DERe1"])
        mset("pool", DER[:, 114:115], 0.25, ["DERe2"])
        mset("pool", DER[:, 115:116], 0.0625, ["DERe"])
        P.last_write["DERe"] = P.last_write["DERe"] + P.last_write["DERe0"] + P.last_write["DERe1"] + P.last_write["DERe2"]

        for j in range(2):
            ts("dve", DER[:, 48 + 8 * j:56 + 8 * j], cvec[:, EB[j] + E_POOLB:EB[j] + E_POOLB + 8], 1.0, None, ALU.mult, None,
               ["cvec"], ["DERa%d" % j])
            tt("dve", DER[:, 48 + 8 * j:56 + 8 * j], DER[:, 48 + 8 * j:56 + 8 * j],
               cvec[:, EB[j] + E_POOLS:EB[j] + E_POOLS + 8], ALU.mult, ["cvec", "DERa%d" % j], ["DERpbs%d" % j])
            act(DER[:, 64 + 12 * j:76 + 12 * j], cvec[:, OB[j] + O_LAM:OB[j] + O_LAM + 12], AF.Exp, ["cvec"], ["DERz%d" % j],
                scale=-1.0)
            ts("dve", DER[:, 128 + 12 * j:140 + 12 * j], cvec[:, OB[j] + O_BRG:OB[j] + O_BRG + 12], 0.5, None, ALU.mult, None,
               ["cvec"], ["DERhb"])
            ts("dve", DER[:, 152 + 12 * j:164 + 12 * j], cvec[:, OB[j] + O_BIG:OB[j] + O_BIG + 12], 0.5, None, ALU.mult, None,
               ["cvec"], ["DERhb"])
        zz = DER[:, 64:88]
        tt_ = DER[:, 88:112]
        NTERM = 10
        ts("dve", tt_, zz, -1.0 / NTERM, 1.0 / (NTERM - 1), ALU.mult, ALU.add, ["DERz0", "DERz1"], ["DERt"])
        for q in range(NTERM - 2, 0, -1):
            tt("dve", tt_, tt_, zz, ALU.mult, ["DERt", "DERz0", "DERz1"], ["DERt"])
            ts("dve", tt_, tt_, -1.0, 1.0 / q, ALU.mult, ALU.add, ["DERt"], ["DERt"])
        tt("dve", tt_, tt_, zz, ALU.mult, ["DERt", "DERz0", "DERz1"], ["DERt"])
        ts("dve", DER[:, 0:24], tt_, -8.0, None, ALU.mult, None, ["DERt"], ["DERsp"])
        ts("dve", DER[:, 24:48], tt_, -4.0, None, ALU.mult, None, ["DERt"], ["DERsp"])

        def load_w(slot, parts):
            def fn(e):
                r = []
                for pt in parts:
                    c0, nc_, src = pt[0], pt[1], pt[2]
                    if len(pt) == 4:
                        dst = WS[slot][:, c0:c0 + 2048].rearrange("p (c n) -> p c n", c=2)
                        r.append(e.dma_start(out=dst, in_=src.rearrange("c p n -> p c n")))
                    else:
                        r.append(e.dma_start(out=WS[slot][:, c0:c0 + nc_], in_=src))
                return r
            P.op("pool", fn, writes=[("W", slot)], dsem=s_w[slot], n=len(parts))

        def build_dg(buf, col0, ntap):
            def fn(e):
                in0 = ident[:].rearrange("p (o j) -> p o j", o=1).to_broadcast([128, ntap, 128])
                in1 = cvec[:, col0:col0 + ntap].rearrange("p (k o) -> p k o", o=1).to_broadcast([128, ntap, 128])
                return e.tensor_tensor(out=DG[buf][:, 0:ntap, :], in0=in0, in1=in1, op=ALU.mult)
            P.op("pool", fn, reads=["ident", "cvec"], writes=[("DG", buf)])

        def build_dg_act(buf, col0, ntap):
            def fn(e):
                return [e.activation(out=DG[buf][:, k, :], in_=ident[:], func=AF.Identity, scale=cvec[:, col0 + k:col0 + k + 1])
                        for k in range(ntap)]
            P.op("act", fn, reads=["ident", "cvec"], writes=[("DG", buf)], n=ntap)

        def rms_stats(hh):
            p = pspair()
            for kc in range(KC):
                for n in range(2):
                    act(SQR[:, tl(n)], H[:, kc, hh * TH + n * NT:hh * TH + (n + 1) * NT], AF.Square, [("H", kc, hh)], [("SQ", n)])
                for n in range(2):
                    def fn(e, kc=kc, n=n):
                        return e.matmul(psb(p, n), onesR[:], SQR[:, tl(n)], start=(kc == 0), stop=(kc == KC - 1))
                    P.op("pe", fn, reads=["onesR", ("SQ", n)], writes=[("ps", 2 * p + n)])
            t, tk = tmpw()
            act(t, psw(p), AF.Ln, pk(p) + ["DERe"], [tk], bias=dc(112))
            act(ST[:, 2, :], t, AF.Exp, [tk], ["RS"], scale=-0.5)

        def norm_half(hh, gcol):
            rms_stats(hh)
            for kc in range(KC):
                stt(HN[:, kc, :], H[:, kc, hh * TH:(hh + 1) * TH], cv(gcol + kc), ST[:, 2, :],
                    ALU.mult, ALU.mult, [("H", kc, hh), "RS", "cvec"], [("HN", kc)])

        def inproj(p, slot, c0):
            for n in range(2):
                mm(p, n, [(WS[slot][:, c0 + kc * 128:c0 + (kc + 1) * 128], HN[:, kc, tl(n)]) for kc in range(KC)],
                   [("W", slot)] + [("HN", kc) for kc in range(KC)])

        steps = []

        def even_half(j, hh):
            cb0 = EB[j]

            def pre(slot, buf):
                P.fence(("REG", "R"))
                P.fence(("REG", "E"))
            steps.append(dict(head=pre, norm=(lambda: norm_half(hh, cb0 + E_NORM)), flush=True))

            for g in (3, 2, 1, 0):
                w = POOL_WINDOWS[g]
                gp = g % 2
                for hf in range(2):
                    cb = 2 * g + hf

                    def dma(slot, g=g, hf=hf, cb=cb):
                        parts = [(0, 2048, wie[j, 16 + 2 * cb:18 + 2 * cb], "pair")]
                        if hf == 1:
                            parts.append((2048, 512, pwd[j, g]))
                        load_w(slot, parts)

                    def head(slot, buf, g=g, hf=hf, cb=cb, gp=gp):
                        VP = VPv(hf)
                        if hh == 0:
                            mset("pool", VP[:, 0:16], 0.0, [("VP", hf, "pad")])
                        else:
                            cp("pool", VP[:, 0:16], VHALO[:, cb, :], [("VHALO", cb)], [("VP", hf, "pad")])
                        pv, pg = pspair(), pspair()
                        inproj(pv, slot, 0)
                        inproj(pg, slot, 1024)
                        act(VP[:, 16:1040], psw(pv), AF.Copy, pk(pv), [("VP", hf, "d")])
                        act(SGBv(gp, hf), psw(pg), AF.Silu, pk(pg), [("SGB", gp, hf)])

                    def tail(slot, buf, g=g, hf=hf, cb=cb, w=w):
                        VP = VPv(hf)
                        vk = [("VP", hf, "pad"), ("VP", hf, "d")]
                        if hh == 0:
                            cp("pool", VHALO[:, cb, :], VP[:, 1024:1040], vk, [("VHALO", cb)])
                        tt("dve", SAv[:, 1:1040], VP[:, 1:1040], VP[:, 0:1039], ALU.add, vk, [("SA",)])
                        cur, curk = SAv, ("SA",)
                        if g >= 1:
                            tt("dve", SBv[:, 3:1040], SAv[:, 3:1040], SAv[:, 1:1038], ALU.add, [("SA",)], [("SB",)])
                            cur, curk = SBv, ("SB",)
                        if g >= 2:
                            tt("dve", SAv[:, 7:1040], SBv[:, 7:1040], SBv[:, 3:1036], ALU.add, [("SB",)], [("SA",)])
                            cur, curk = SAv, ("SA",)
                        if g >= 3:
                            tt("dve", SBv[:, 15:1040], SAv[:, 15:1040], SAv[:, 7:1032], ALU.add, [("SA",)], [("SB",)])
                            cur, curk = SBv, ("SB",)
                        Dh = Dv(hf)
                        stt(Dh[:, :], cur[:, 16:1040], 1.0 / w, VP[:, 16:1040], ALU.mult, ALU.subtract,
                            [curk] + vk, [("Dp", hf)])
                        if hh == 0:
                            t, tk = tmpw()
                            tt("dve", t[:, 0:16], cur[:, 16:32], ctab[:, 16 * g:16 * g + 16], ALU.mult, [curk, "ctab"], [tk])
                            tt("dve", Dh[:, 0:16], t[:, 0:16], VP[:, 16:32], ALU.subtract, [tk] + vk + [("Dp", hf)],
                               [("Dp", hf)])

                    def deferred(slot, buf, g=g, gp=gp):
                        for e_ in range(2):
                            pp = pspair()
                            for n in range(2):
                                mm(pp, n, [(WS[slot][:, 2048 + h2 * 256 + e_ * 128:2048 + h2 * 256 + (e_ + 1) * 128],
                                            Dv(h2)[:, tl(n)]) for h2 in range(2)],
                                   [("W", slot), ("Dp", 0), ("Dp", 1)])
                            t, tk = tmpw()
                            act(t, psw(pp), AF.Identity, pk(pp) + ["cvec", "DERpbs%d" % j], [tk],
                                scale=cv(cb0 + E_POOLS + 2 * g + e_), bias=dc(48 + 8 * j + 2 * g + e_))
                            tt("dve", YBv(2 * g + e_)[:, :], t, SGBv(gp, e_), ALU.mult,
                               [tk, ("SGB", gp, e_)], [("YB", 2 * g + e_)])
                    steps.append(dict(dma=dma, head=head, tail=tail, deferred=(deferred if hf == 1 else None)))

            a1box = {}
            for ca in range(8):
                def dma(slot, ca=ca):
                    load_w(slot, [(0, 2048, wie[j, 2 * ca:2 * ca + 2], "pair")])

                def aux(buf, ca=ca):
                    col0 = cb0 + E_CONVW + ca * 32

                    def fn(e):
                        in0 = I4[:].rearrange("p (o c) -> p o c", o=1).to_broadcast([128, 32, 32])
                        in1 = cvec[:, col0:col0 + 32].rearrange("p (q o) -> p q o", o=1).to_broadcast([128, 32, 32])
                        return e.tensor_tensor(out=LT[buf][:, :, :], in0=in0, in1=in1, op=ALU.mult)
                    P.op("pool", fn, reads=["I4", "cvec"], writes=[("LT", buf)])

                def head(slot, buf, ca=ca):
                    if ca <= 1:
                        P.fence(("REG", "R"))
                    if ca == 0:
                        mset("pool", ST[:, 0, :], 0.0, ["ACC1"])
                        mset("pool", ST[:, 1, :], 0.0, ["ACC2"])
                    ub = UB[buf]

                    def ub_pad(b_, c_):
                        if hh == 0:
                            mset("pool", UB[b_][:, 0:32], 0.0, [("UB", b_, "pad")])
                        else:
                            cp("pool", UB[b_][:, 0:32], UHALO[:, c_, :], [("UHALO", c_)], [("UB", b_, "pad")])
                    if ca == 0:
                        ub_pad(buf, 0)
                    if ca + 1 < 8:
                        ub_pad(1 - buf, ca + 1)
                    pg, pv = pspair(), pspair()
                    inproj(pg, slot, 1024)
                    sg, sgk = tmpw()
                    act(sg, psw(pg), AF.Sigmoid, pk(pg), [sgk])
                    inproj(pv, slot, 0)
                    tt("dve", ub[:, 32:32 + TH], psw(pv), sg, ALU.mult, pk(pv) + [sgk], [("UB", buf, "d")])

                def conv(buf, ca):
                    pc = pspair()

                    def fn(e):
                        r = None
                        for n in range(2):
                            for g in range(8):
                                for jb in range(4):
                                    r = e.matmul(PSALL[32 * jb:32 * jb + 32, 2 * pc + n, :], LT[buf][:, jb * 8 + g, :],
                                                 SST[:, jb, 2 + 4 * g + n * NT:2 + 4 * g + (n + 1) * NT],
                                                 start=(g == 0), stop=(g == 7), tile_position=(0, 32 * jb))
                        return r
                    P.op("pe", fn, reads=[("LT", buf), ("SST", "a"), ("SST", "b")], writes=pk(pc))
                    ts("dve", Uv(ca), psw(pc), cv(cb0 + E_CONVB + ca), None, ALU.add, None, pk(pc) + ["cvec"],
                       [("U", ca), ("YA", ca)])
                    us, usk = tmpw()
                    act(us, Uv(ca), AF.Square, [("U", ca)], [usk])
                    tt("dve", ST[:, 0, :], ST[:, 0, :], Uv(ca), ALU.add, ["ACC1", ("U", ca)], ["ACC1"])
                    tt("dve", ST[:, 1, :], ST[:, 1, :], us, ALU.add, ["ACC2", usk], ["ACC2"])

                def tail(slot, buf, ca=ca):
                    ub = UB[buf]
                    ubk = [("UB", buf, "pad"), ("UB", buf, "d"), ("UB", buf, "tail")]

                    def fs(jbs):
                        return lambda e: [e.dma_start(out=SST[32 * kk:32 * kk + 32, jb, :], in_=ub[32 * jb:32 * jb + 32, kk:kk + 1056])
                                          for jb in jbs for kk in range(4)]
                    P.op("sp", fs((0, 1)), reads=ubk, writes=[("SST", "a")], dsem=s_s, n=8)
                    P.op("pool", fs((2, 3)), reads=ubk, writes=[("SST", "b")], dsem=s_s2, n=8)
                    if hh == 0:
                        cp("pool", UHALO[:, ca, :], ub[:, 1024:1056], [("UB", buf, "d")], [("UHALO", ca)])
                    if ca == 7:
                        conv(buf, ca)
                        p1, p2 = pspair(), pspair()
                        for n in range(2):
                            P.op("pe", lambda e, n=n: e.matmul(psb(p1, n), onesD[:], ST[:, 0, tl(n)], start=True, stop=True),
                                 reads=["onesD", "ACC1"], writes=[("ps", 2 * p1 + n)])
                            P.op("pe", lambda e, n=n: e.matmul(psb(p2, n), onesD[:], ST[:, 1, tl(n)], start=True, stop=True),
                                 reads=["onesD", "ACC2"], writes=[("ps", 2 * p2 + n)])
                        cp("dve", ST[:, 0, :], psw(p1), pk(p1), ["ACC1", "MU"])
                        t, tk = tmpw()
                        tt("dve", t, ST[:, 0, :], ST[:, 0, :], ALU.mult, ["MU"], [tk])
                        tt("dve", t, psw(p2), t, ALU.subtract, pk(p2) + [tk], [tk])
                        act(t, t, AF.Ln, [tk, "DERe"], [tk], bias=dc(113))
                        act(ST[:, 1, :], t, AF.Exp, [tk], ["ACC2", "RSTD"], scale=-0.5)
                steps.append(dict(dma=dma, aux=aux, head=head, tail=tail, aux_late=True,
                                  deferred=((lambda slot, buf, ca=ca: conv(buf, ca)) if ca < 7 else None)))

            a2box = {}
            for ca in range(8):
                def dma(slot, ca=ca):
                    load_w(slot, [(0, 1024, wie[j, 32 + ca])])

                def head(slot, buf, ca=ca, box=None):
                    pg = pspair()
                    inproj(pg, slot, 0)
                    sg, sgk = tmpw()
                    act(sg, psw(pg), AF.Silu, pk(pg), [sgk])
                    t1, t1k = tmpw()
                    tt("dve", t1, Uv(ca), ST[:, 0, :], ALU.subtract, [("U", ca), "MU"], [t1k])
                    tt("dve", t1, t1, ST[:, 1, :], ALU.mult, [t1k, "RSTD"], [t1k])
                    act(t1, t1, AF.Silu, [t1k, "cvec"], [t1k], scale=cv(cb0 + E_LNG + ca), bias=cv(cb0 + E_LNB + ca))
                    a2box[ca] = (sg, sgk, t1, t1k)

                def deferred(slot, buf, ca=ca):
                    sg, sgk, t1, t1k = a2box[ca]
                    tt("dve", YAv(ca)[:, :], t1, sg, ALU.mult, [t1k, sgk], [("YA", ca), ("U", ca)])
                steps.append(dict(dma=dma, head=head, deferred=deferred))

            obox = {}
            for mo in range(8):
                def dma(slot, mo=mo):
                    load_w(slot, [(0, 2048, woe[j, mo])])

                def head(slot, buf, mo=mo):
                    if mo == 0:
                        P.fence(("REG", "E"))
                    po = pspair()
                    obox[mo] = po
                    for n in range(2):
                        pairs = [(WS[slot][:, (8 + kc) * 128:(9 + kc) * 128], YBv(kc)[:, tl(n)]) for kc in range(8)]
                        pairs += [(WS[slot][:, kc * 128:(kc + 1) * 128], YAv(kc)[:, tl(n)]) for kc in range(6)]
                        mm_part(po, n, pairs, [("W", slot)] + [("YA", kc) for kc in range(6)] + [("YB", kc) for kc in range(8)],
                                first=True, last=False)

                def deferred2(slot, buf, mo=mo):
                    po = obox[mo]
                    for n in range(2):
                        pairs = [(WS[slot][:, kc * 128:(kc + 1) * 128], YAv(kc)[:, tl(n)]) for kc in (6, 7)]
                        mm_part(po, n, pairs, [("W", slot), ("YA", 6), ("YA", 7)], first=False, last=True)
                    hs = H[:, mo, hh * TH:(hh + 1) * TH]
                    tt("dve", hs, psw(po), hs, ALU.add, pk(po) + [("H", mo, hh)], [("H", mo, hh)])
                steps.append(dict(dma=dma, head=head, deferred=deferred2, flush=(mo == 0), ofirst=(mo == 0)))

        def odd_half(j, hh):
            ob = OB[j]

            def pre(slot, buf):
                P.fence(("REG", "R"))
                P.fence(("REG", "E"))
            steps.append(dict(head=pre, norm=(lambda: norm_half(hh, ob + O_NORM)), flush=True))

            for m in range(NHEAD):
                def dma(slot, m=m):
                    load_w(slot, [(0, 2048, wio[j, 2 * m:2 * m + 2], "pair"), (2048, 256, wgt[j, m])])

                def aux(buf, m=m):
                    build_dg(buf, ob + O_CCW + 4 * m, 4)

                def xr_pad(b_, m_):
                    xr_ = XRv(b_)
                    if hh == 0:
                        mset("pool", xr_[:, 0:8], 0.0, [("XR", b_, "pad")])
                    else:
                        cp("pool", xr_[:, 4:8], XHALO[:, m_, :], [("XHALO", m_)], [("XR", b_, "pad")])

                def head(slot, buf, m=m):
                    xr = XRv(buf)
                    if m == 0:
                        xr_pad(buf, 0)
                    if m + 1 < NHEAD:
                        xr_pad(1 - buf, m + 1)
                    pg = pspair()
                    inproj(pg, slot, 1024)
                    SG = SGv(m % 4)
                    act(SG, psw(pg), AF.Tanh, pk(pg), [("SG", m % 4)], scale=0.5)
                    stt(SG, SG, 1.0, psw(pg), ALU.add, ALU.mult, [("SG", m % 4)] + pk(pg), [("SG", m % 4)])
                    px = pspair()
                    inproj(px, slot, 0)
                    cp("dve", xr[:, 8:8 + TH], psw(px), pk(px), [("XR", buf, "d")])
                    if hh == 0:
                        cp("dve", XHALO[:, m, :], xr[:, 1028:1032], [("XR", buf, "d")], [("XHALO", m)])

                def tail(slot, buf, m=m):
                    xr = XRv(buf)
                    pc = pspair()
                    for n in range(2):
                        mm(pc, n, [(DG[buf][:, k, :], xr[:, 5 + n * NT + k:5 + n * NT + k + NT]) for k in range(4)],
                           [("DG", buf), ("XR", buf, "pad"), ("XR", buf, "d")])
                    ts("dve", TXv(buf), psw(pc), cv(ob + O_CCB + m), None, ALU.add, None, pk(pc) + ["cvec"], [("TX", buf)])
                    ts("dve", XCBv(buf), psw(pc), cv(ob + O_CCB + m), None, ALU.add, None, pk(pc) + ["cvec"], [("XCB", buf)])

                def deferred(slot, buf, m=m):
                    TR, TRk = TMP[:, 0, :], ("TMP", 0)
                    ti = 1 if buf == 0 else 3
                    TI, TIk = TMP[:, ti, :], ("TMP", ti)
                    TE, TEk = TMP[:, 2, :], ("TMP", 2)
                    pr, pi = pspair(), pspair()
                    for n in range(2):
                        mm(pr, n, [(WS[slot][:, 2048:2176], XCBv(buf)[:, tl(n)])], [("W", slot), ("XCB", buf)])
                        mm(pi, n, [(WS[slot][:, 2176:2304], XCBv(buf)[:, tl(n)])], [("W", slot), ("XCB", buf)])
                    act(TR, psw(pr), AF.Tanh, pk(pr) + ["DERhb"], [TRk], scale=0.5, bias=dc(128 + 12 * j + m))
                    act(TI, psw(pi), AF.Tanh, pk(pi) + ["DERhb"], [TIk], scale=0.5, bias=dc(152 + 12 * j + m))
                    act(TE, TR, AF.Exp, [TRk, "DERsp"], [TEk], scale=dc(12 * j + m), bias=dc(12 * j + m))
                    act(Av(buf), TR, AF.Exp, [TRk, "DERsp"], [("A", buf)], scale=dc(24 + 12 * j + m), bias=dc(24 + 12 * j + m))
                    act(TE, TE, AF.Relu, [TEk, "DERe"], [TEk], scale=-0.0625, bias=dc(115))
                    act(TE, TE, AF.Sqrt, [TEk], [TEk])
                    act(DER[:, 120:121], DER[:, 114:115], AF.Tanh, ["DERe"], [("DUM",)])
                    stt(TI, TI, 1.0, TXv(buf), ALU.add, ALU.mult, [TIk, ("TX", buf)], [TIk])
                    tt("dve", TI, TI, TE, ALU.mult, [TIk, TEk], [TIk])

                def deferred2(slot, buf, m=m):
                    ti = 1 if buf == 0 else 3
                    TI, TIk = TMP[:, ti, :], ("TMP", ti)
                    init = 0.0 if hh == 0 else CARRY[:, m:m + 1]
                    P.op("dve", lambda e: e.tensor_tensor_scan(out=HSv, data0=Av(buf), data1=TI, initial=init,
                                                               op0=ALU.mult, op1=ALU.add),
                         reads=[("A", buf), TIk, ("CARRY", m)], writes=[("HS",)])
                    if hh == 0:
                        cp("dve", CARRY[:, m:m + 1], HSv[:, TH - 1:TH], [("HS",)], [("CARRY", m)])
                    tt("dve", Yv(m)[:, :], SGv(m % 4), HSv, ALU.mult, [("SG", m % 4), ("HS",)], [("Y", m)])
                steps.append(dict(dma=dma, aux=aux, head=head, tail=tail, deferred=deferred, deferred2=deferred2))

            oobox = {}
            for mo in range(8):
                def dma(slot, mo=mo):
                    load_w(slot, [(0, 1536, woo[j, mo])])

                def head(slot, buf, mo=mo):
                    if mo == 0:
                        P.fence(("REG", "E"))
                    po = pspair()
                    oobox[mo] = po
                    for n in range(2):
                        mm_part(po, n, [(WS[slot][:, kc * 128:(kc + 1) * 128], Yv(kc)[:, tl(n)]) for kc in range(NHEAD - 2)],
                                [("W", slot)] + [("Y", kc) for kc in range(NHEAD - 2)], first=True, last=False)

                def deferred2(slot, buf, mo=mo):
                    po = oobox[mo]
                    for n in range(2):
                        mm_part(po, n, [(WS[slot][:, kc * 128:(kc + 1) * 128], Yv(kc)[:, tl(n)]) for kc in (NHEAD - 2, NHEAD - 1)],
                                [("W", slot), ("Y", NHEAD - 2), ("Y", NHEAD - 1)], first=False, last=True)
                    hs = H[:, mo, hh * TH:(hh + 1) * TH]
                    tt("dve", hs, psw(po), hs, ALU.add, pk(po) + [("H", mo, hh)], [("H", mo, hh)])
                steps.append(dict(dma=dma, head=head, deferred=deferred2, flush=(mo == 0), ofirst=(mo == 0)))

        def final_norm_half(hh):
            def fn(slot, buf):
                rms_stats(hh)
                for kc in range(KC):
                    i = kc % 2
                    stg = ST[:, i, :]
                    stt(stg, H[:, kc, hh * TH:(hh + 1) * TH], cv(FN_BASE + kc), ST[:, 2, :], ALU.mult, ALU.mult,
                        [("H", kc, hh), "RS", "cvec"], [("ST", i)])
                    P.op("sp", lambda e, kc=kc, stg=stg: e.dma_start(out=outd[kc * 128:(kc + 1) * 128, hh * TH:(hh + 1) * TH], in_=stg),
                         reads=[("ST", i)], dsem=s_o[i])
            steps.append(dict(head=fn, flush=True))

        def store_h_half(hh):
            def fn(slot, buf):
                for kc in range(KC):
                    P.op("sp", lambda e, kc=kc: e.dma_start(out=outd[kc * 128:(kc + 1) * 128, hh * TH:(hh + 1) * TH],
                                                            in_=H[:, kc, hh * TH:(hh + 1) * TH]),
                         reads=[("H", kc, hh)], dsem=s_o[kc % 2])
            steps.append(dict(head=fn, flush=True))

        for L in layers:
            for hh in range(2):
                if L % 2 == 0:
                    even_half(L // 2, hh)
                else:
                    odd_half(L // 2, hh)
        for hh in range(2):
            if do_final_norm:
                final_norm_half(hh)
            else:
                store_h_half(hh)

        wl = [i for i, s_ in enumerate(steps) if s_.get("dma") is not None]
        pos = {i: q for q, i in enumerate(wl)}
        sb_of = {i: (q % 3, q % 2) for q, i in enumerate(wl)}
        issued = {"dma": 0, "aux": 0}

        def issue(kind, upto):
            while issued[kind] <= upto and issued[kind] < len(wl):
                i = wl[issued[kind]]
                f = steps[i].get(kind)
                if f is not None:
                    f(sb_of[i][0] if kind == "dma" else sb_of[i][1])
                issued[kind] += 1

        issue("dma", 1)
        issue("aux", 0)
        pend = {"A": None, "B0": None, "B1": None}

        def run(k):
            if pend[k] is not None:
                pend[k]()
                pend[k] = None

        for i, s_ in enumerate(steps):
            slot, buf = sb_of.get(i, (None, None))
            if s_.get("flush"):
                run("A")
                run("B0")
                run("B1")
            s_["head"](slot, buf)
            if s_.get("norm") is not None and not s_.get("norm_done"):
                s_["norm"]()
                s_["norm_done"] = True
            if s_.get("ofirst"):
                for s2 in steps[i + 1:]:
                    if s2.get("norm") is not None:
                        if not s2.get("norm_done"):
                            s2["norm"]()
                            s2["norm_done"] = True
                        break
            run("B0")
            run("A")
            if i in pos:
                issue("dma", pos[i] + 2)
                if not s_.get("aux_late"):
                    issue("aux", pos[i] + 1)
            if s_.get("tail") is not None:
                s_["tail"](slot, buf)
            if i in pos and s_.get("aux_late"):
                issue("aux", pos[i] + 1)
            pend["B0"], pend["B1"] = pend["B1"], None
            if s_.get("deferred") is not None:
                pend["A"] = (lambda f=s_["deferred"], slot=slot, buf=buf: f(slot, buf))
            if s_.get("deferred2") is not None:
                pend["B1"] = (lambda f=s_["deferred2"], slot=slot, buf=buf: f(slot, buf))
        run("A")
        run("B0")
        run("B1")

        P.final_wait("sp", s_o)
        with nc.Block() as block:
            P.replay(block)
    return nc


def _chunk_cols(v):
    v = np.asarray(v, np.float32)
    return np.ascontiguousarray(v.reshape(-1, 128).T)


def _prep_shared(inp):
    f = lambda a: np.asarray(a, np.float32)
    cvec = np.zeros((128, NCV), np.float32)
    for j in range(2):
        b = EB[j]
        cvec[:, b + E_NORM:b + E_NORM + 8] = _chunk_cols(f(inp["norm_even"])[j])
        cvec[:, b + E_CONVB:b + E_CONVB + 8] = _chunk_cols(f(inp["conv_a_b"])[j])
        cvec[:, b + E_LNG:b + E_LNG + 8] = _chunk_cols(f(inp["ln_a_g"])[j])
        cvec[:, b + E_LNB:b + E_LNB + 8] = _chunk_cols(f(inp["ln_a_b"])[j])
        cvec[:, b + E_POOLB:b + E_POOLB + 8] = _chunk_cols(f(inp["pool_b"])[j].reshape(-1))
        cvec[:, b + E_POOLS:b + E_POOLS + 8] = _chunk_cols(f(inp["pool_scale"])[j])
        cw = np.concatenate([f(inp["conv_a_w"])[j], np.zeros((1, 1024), np.float32)], axis=0)
        cvec[:, b + E_CONVW:b + E_CONVW + 256] = cw.reshape(8, 4, 8, 4, 32).transpose(1, 4, 2, 3, 0).reshape(128, 256)
        o = OB[j]
        cvec[:, o + O_NORM:o + O_NORM + 8] = _chunk_cols(f(inp["norm_odd"])[j])
        cvec[:, o + O_CCB:o + O_CCB + 12] = _chunk_cols(f(inp["conv_c_b"])[j])
        cvec[:, o + O_BRG:o + O_BRG + 12] = _chunk_cols(f(inp["b_rg"])[j])
        cvec[:, o + O_BIG:o + O_BIG + 12] = _chunk_cols(f(inp["b_ig"])[j])
        cvec[:, o + O_LAM:o + O_LAM + 12] = _chunk_cols(f(inp["lru_lambda"])[j])
        ccw = f(inp["conv_c_w"])[j]
        cvec[:, o + O_CCW:o + O_CCW + 48] = ccw.reshape(4, 12, 128).transpose(2, 1, 0).reshape(128, 48)
    cvec[:, FN_BASE:FN_BASE + 8] = _chunk_cols(f(inp["final_norm"]))
    ctab = np.zeros((128, 64), np.float32)
    for g, w in enumerate(POOL_WINDOWS):
        ctab[:, 16 * g:16 * g + 16] = 1.0 / np.minimum(np.arange(1, 17), w).astype(np.float32)
    wie = f(inp["w_in_even"]).reshape(2, 8, 128, 40, 128).transpose(0, 3, 2, 1, 4).reshape(2, 40, 128, 1024)
    order_e = [c for ca in range(8) for c in (ca, 8 + ca)] + [c for cb in range(8) for c in (24 + cb, 32 + cb)] + [16 + ca for ca in range(8)]
    wie = np.ascontiguousarray(wie[:, order_e])
    woe = np.ascontiguousarray(f(inp["w_out_even"]).reshape(2, 16, 128, 8, 128).transpose(0, 3, 2, 1, 4).reshape(2, 8, 128, 2048))
    wio = f(inp["w_in_odd"]).reshape(2, 8, 128, 24, 128).transpose(0, 3, 2, 1, 4).reshape(2, 24, 128, 1024)
    order_o = [c for m in range(12) for c in (m, 12 + m)]
    wio = np.ascontiguousarray(wio[:, order_o])
    woo = np.ascontiguousarray(f(inp["w_out_odd"]).reshape(2, 12, 128, 8, 128).transpose(0, 3, 2, 1, 4).reshape(2, 8, 128, 1536))
    pw = np.ascontiguousarray(f(inp["pool_w"]).reshape(2, 4, 2, 128, 2, 128).transpose(0, 1, 3, 2, 4, 5).reshape(2, 4, 128, 512))
    wgt = np.ascontiguousarray(np.concatenate([f(inp["w_rg"]), f(inp["w_ig"])], axis=-1))
    return {"cvec": cvec, "ctab": ctab, "wie": wie, "woe": woe, "wio": wio, "woo": woo, "pw": pw, "wgt": wgt}


_PROGS = {}


def _get_prog(layers, fin):
    key = (tuple(layers), fin)
    if key not in _PROGS:
        _PROGS[key] = build_program(list(layers), fin)
    return _PROGS[key]


LAUNCH_PLAN = [((0, 1, 2, 3), True)]


def kernel(**inputs):
    shared = _prep_shared(inputs)
    x = np.asarray(inputs["x"], np.float32)
    n = x.shape[0]
    cur = [np.ascontiguousarray(x[b].T) for b in range(n)]
    for (layers, fin) in LAUNCH_PLAN:
        nc = _get_prog(layers, fin)
        in_maps = [dict(shared, xin=cur[b]) for b in range(n)]
        res = run_bass_kernel_spmd(nc, in_maps, core_ids=list(range(n)))
        cur = [np.asarray(res.results[b]["out"], np.float32) for b in range(n)]
    return np.stack([c.T for c in cur], axis=0).astype(np.float32)
```
